# Optimizing a Trainium2 kernel written in Bass

```python
import math
import jax, jax.numpy as jnp
from jax import lax
import numpy as np

D_MODEL = 1024
BATCH = 8
SEQ = 2048
DEPTH = 4

CHUNK = 64
N_META = 16
Q_BLOCK = 128
N_MIXERS = 3
NEG_INF = -1e30

REL_BUCKETS = 32
REL_MAX_DIST = 128
REL_HEADS = 16

D_FF = 2816

A_HEADS = 8
A_HD = 64
A_VD = 2 * A_HD

B_HEADS = 16
B_Q_RANK = 256
B_KV_RANK = 256
B_VD = 64
IDX_HEADS = 8
IDX_DIM = 64
TOPK_MAX = 256

C_Q_HEADS = 16
C_KV_HEADS = 2
C_GROUP = C_Q_HEADS // C_KV_HEADS
C_HD = 64
WINDOW = 128
WINDOW_CHUNKS = -(-WINDOW // CHUNK)
C_BEHIND = WINDOW_CHUNKS * CHUNK

kernel_name = "chunked_hybrid_diff_dsa_swa_macaron"


def _rms_norm(x, g=None, eps=1e-6):
    xf = x.astype(jnp.float32)
    y = xf * lax.rsqrt(jnp.mean(xf * xf, axis=-1, keepdims=True) + eps)
    if g is not None:
        y = y * g.astype(jnp.float32)
    return y.astype(x.dtype)


def _swiglu(x, wi, wo):
    a, b = jnp.split(x @ wi, 2, axis=-1)
    return (jax.nn.silu(a) * b) @ wo


def _chunk_id(p):
    return jnp.where(p < N_META, 0, 1 + (p - N_META) // CHUNK)


def _rel_bucket(rel):
    half = REL_BUCKETS // 2
    max_exact = half // 2
    n = jnp.abs(rel)
    large = max_exact + (jnp.log(jnp.maximum(n, 1).astype(jnp.float32) / max_exact)
                         / math.log(REL_MAX_DIST / max_exact) * (half - max_exact)).astype(jnp.int32)
    large = jnp.minimum(large, half - 1)
    return jnp.where(rel > 0, half, 0) + jnp.where(n < max_exact, n, large)


def _rel_bias(rel_bias, rel):
    return rel_bias[_rel_bucket(rel)]


def _to_blocks(a):
    b, s = a.shape[:2]
    return jnp.moveaxis(a.reshape(b, s // Q_BLOCK, Q_BLOCK, *a.shape[2:]), 1, 0)


def _from_blocks(a):
    nb, b = a.shape[:2]
    return jnp.moveaxis(a, 0, 1).reshape(b, nb * Q_BLOCK, *a.shape[3:])


def _diff_attention(h, pos, cid, w_in, qk_norm, lam, subln, lambda_init, rel_bias):
    bsz, t, _ = h.shape
    q, k, v = jnp.split(h @ w_in, [2 * A_HEADS * A_HD, 4 * A_HEADS * A_HD], axis=-1)
    q = _rms_norm(q.reshape(bsz, t, A_HEADS, 2, A_HD), qk_norm[0])
    k = _rms_norm(k.reshape(bsz, t, A_HEADS, 2, A_HD), qk_norm[1])
    v = v.reshape(bsz, t, A_HEADS, A_VD)
    lf = lam.astype(jnp.float32)
    lam_full = jnp.exp(jnp.dot(lf[0], lf[1])) - jnp.exp(jnp.dot(lf[2], lf[3])) + lambda_init
    scale = A_HD ** -0.5

    def block(qb, qpos):
        nq = qpos.shape[0]
        bias = _rel_bias(rel_bias, pos[None, :] - qpos[:, None])
        bias = bias.reshape(nq, t, 2, A_HEADS).transpose(2, 3, 0, 1)
        logits = jnp.einsum("bqhmd,bkhmd->bmhqk", qb, k).astype(jnp.float32) * scale + bias
        visible = cid[None, :] <= _chunk_id(qpos)[:, None]
        p = jax.nn.softmax(jnp.where(visible, logits, NEG_INF), axis=-1)
        pd = p[:, 0] - lam_full * p[:, 1]
        return jnp.einsum("bhqk,bkhe->bqhe", pd.astype(v.dtype), v)

    o_meta = block(q[:, :N_META], pos[:N_META])
    o_real = _from_blocks(lax.map(lambda a: block(*a),
                                  (_to_blocks(q[:, N_META:]), pos[N_META:].reshape(-1, Q_BLOCK))))
    o = jnp.concatenate([o_meta, o_real], axis=1)
    o = _rms_norm(o, subln) * (1.0 - lambda_init)
    return o.reshape(bsz, t, A_HEADS * A_VD)


def _dsa_attention(h, pos, cid, w_in, latent_norm, w_uq, q_norm, w_uv, rel_bias, k_sel):
    bsz, t, _ = h.shape
    c_q, c_kv, k_idx, w_idx = jnp.split(
        h @ w_in, [B_Q_RANK, B_Q_RANK + B_KV_RANK, B_Q_RANK + B_KV_RANK + IDX_DIM], axis=-1)
    c_q = _rms_norm(c_q, latent_norm[0])
    c_kv = _rms_norm(c_kv, latent_norm[1])
    q_abs, q_idx = jnp.split(c_q @ w_uq, [B_HEADS * B_KV_RANK], axis=-1)
    q_abs = _rms_norm(q_abs.reshape(bsz, t, B_HEADS, B_KV_RANK), q_norm)
    q_idx = q_idx.reshape(bsz, t, IDX_HEADS, IDX_DIM)
    k_idx = _rms_norm(k_idx)
    w_idx = w_idx * IDX_HEADS ** -0.5
    gather = jax.vmap(lambda src, idx: src[idx])

    def block(qa, qi, wi, qpos):
        cid_q = _chunk_id(qpos)
        act = jax.nn.relu(jnp.einsum("bqhd,bsd->bqhs", qi, k_idx) * IDX_DIM ** -0.5)
        score = jnp.einsum("bqh,bqhs->bqs", wi, act).astype(jnp.float32)
        score = jnp.where(cid[None, :] <= cid_q[:, None], score, NEG_INF)
        _, sel = lax.top_k(score, k_sel)
        kv = gather(c_kv, sel)
        valid = cid[sel] <= cid_q[None, :, None]
        bias = _rel_bias(rel_bias, pos[sel] - qpos[None, :, None])
        logits = (jnp.einsum("bqhr,bqkr->bqhk", qa, kv).astype(jnp.float32) * B_KV_RANK ** -0.5
                  + jnp.swapaxes(bias, -1, -2))
        p = jax.nn.softmax(jnp.where(valid[:, :, None, :], logits, NEG_INF), axis=-1)
        o_lat = jnp.einsum("bqhk,bqkr->bqhr", p.astype(kv.dtype), kv)
        o = jnp.einsum("bqhr,hrd->bqhd", o_lat, w_uv)
        return o.reshape(o.shape[0], o.shape[1], B_HEADS * B_VD)

    o_meta = block(q_abs[:, :N_META], q_idx[:, :N_META], w_idx[:, :N_META], pos[:N_META])
    o_real = _from_blocks(lax.map(lambda a: block(*a),
                                  (_to_blocks(q_abs[:, N_META:]), _to_blocks(q_idx[:, N_META:]),
                                   _to_blocks(w_idx[:, N_META:]), pos[N_META:].reshape(-1, Q_BLOCK))))
    return jnp.concatenate([o_meta, o_real], axis=1)


def _swa_attention(h, w_in, qk_norm, sinks, rel_bias):
    bsz, t, _ = h.shape
    q, k, v = jnp.split(h @ w_in, [C_Q_HEADS * C_HD, (C_Q_HEADS + C_KV_HEADS) * C_HD], axis=-1)
    q = _rms_norm(q.reshape(bsz, t, C_Q_HEADS, C_HD), qk_norm[0])
    k = _rms_norm(k.reshape(bsz, t, C_KV_HEADS, C_HD), qk_norm[1])
    v = v.reshape(bsz, t, C_KV_HEADS, C_HD)
    sink = sinks.astype(jnp.float32).reshape(C_KV_HEADS, C_GROUP, 1, 1)

    def attend(qb, kb, vb, qpos, kpos, visible):
        nq, nk = qb.shape[1], kb.shape[1]
        bias = _rel_bias(rel_bias, kpos[None, :] - qpos[:, None])
        bias = bias.transpose(2, 0, 1).reshape(C_KV_HEADS, C_GROUP, nq, nk)
        qg = qb.reshape(bsz, nq, C_KV_HEADS, C_GROUP, C_HD)
        logits = jnp.einsum("bqgjd,bkgd->bgjqk", qg, kb).astype(jnp.float32) * C_HD ** -0.5 + bias
        logits = jnp.where(visible, logits, NEG_INF)
        logits = jnp.concatenate([logits, jnp.broadcast_to(sink, logits.shape[:-1] + (1,))], axis=-1)
        p = jax.nn.softmax(logits, axis=-1)[..., :-1]
        o = jnp.einsum("bgjqk,bkgd->bqgjd", p.astype(vb.dtype), vb)
        return o.reshape(bsz, nq, C_Q_HEADS * C_HD)

    meta_pos = jnp.arange(N_META)
    o_meta = attend(q[:, :N_META], k[:, :N_META], v[:, :N_META], meta_pos, meta_pos,
                    jnp.ones((N_META, N_META), dtype=bool))
    pad = ((0, 0), (C_BEHIND, 0), (0, 0), (0, 0))
    k_pad = jnp.pad(k[:, N_META:], pad)
    v_pad = jnp.pad(v[:, N_META:], pad)
    band = Q_BLOCK + C_BEHIND

    def real_block(qb, blk):
        start = blk * Q_BLOCK
        fq = start + jnp.arange(Q_BLOCK)
        fk = start - C_BEHIND + jnp.arange(band)
        kb = jnp.concatenate([k[:, :N_META], lax.dynamic_slice_in_dim(k_pad, start, band, axis=1)], axis=1)
        vb = jnp.concatenate([v[:, :N_META], lax.dynamic_slice_in_dim(v_pad, start, band, axis=1)], axis=1)
        cq, ck = fq // CHUNK, fk // CHUNK
        in_window = ((fk[None, :] >= 0) & (ck[None, :] <= cq[:, None])
                     & (ck[None, :] >= cq[:, None] - WINDOW_CHUNKS))
        visible = jnp.concatenate([jnp.ones((Q_BLOCK, N_META), dtype=bool), in_window], axis=1)
        kpos = jnp.concatenate([meta_pos, N_META + fk])
        return attend(qb, kb, vb, N_META + fq, kpos, visible)

    nblk = (t - N_META) // Q_BLOCK
    o_real = _from_blocks(lax.map(lambda a: real_block(*a), (_to_blocks(q[:, N_META:]), jnp.arange(nblk))))
    return jnp.concatenate([o_meta, o_real], axis=1)


def setup_inputs(seed: int = 0) -> dict:
    key = jax.random.key(seed)
    ks = iter(jax.random.split(key, 32))
    f32 = jnp.float32
    n_a = len(range(0, DEPTH, N_MIXERS))
    n_b = len(range(1, DEPTH, N_MIXERS))
    n_c = len(range(2, DEPTH, N_MIXERS))

    def w(shape, fan_in):
        return jax.random.normal(next(ks), shape, f32) * fan_in ** -0.5

    def gain(shape):
        return 1.0 + 0.05 * jax.random.normal(next(ks), shape, f32)

    def rnd(shape, s):
        return s * jax.random.normal(next(ks), shape, f32)

    return {
        "x": jax.random.normal(next(ks), (BATCH, SEQ, D_MODEL), f32),
        "meta_tokens": rnd((N_META, D_MODEL), 1.0),
        "rel_bias": rnd((REL_BUCKETS, REL_HEADS), 0.5),
        "ln_ffn1": gain((DEPTH, D_MODEL)),
        "ffn1_wi": w((DEPTH, D_MODEL, 2 * D_FF), D_MODEL),
        "ffn1_wo": w((DEPTH, D_FF, D_MODEL), D_FF),
        "ln_mix": gain((DEPTH, D_MODEL)),
        "w_out": w((DEPTH, D_MODEL, D_MODEL), D_MODEL),
        "ln_ffn2": gain((DEPTH, D_MODEL)),
        "ffn2_wi": w((DEPTH, D_MODEL, 2 * D_FF), D_MODEL),
        "ffn2_wo": w((DEPTH, D_FF, D_MODEL), D_FF),
        "a_w_in": w((n_a, D_MODEL, 4 * A_HEADS * A_HD + A_HEADS * A_VD), D_MODEL),
        "a_qk_norm": gain((n_a, 2, A_HD)),
        "a_lambda": rnd((n_a, 4, A_HD), 0.1),
        "a_subln": gain((n_a, A_VD)),
        "b_w_in": w((n_b, D_MODEL, B_Q_RANK + B_KV_RANK + IDX_DIM + IDX_HEADS), D_MODEL),
        "b_latent_norm": gain((n_b, 2, B_KV_RANK)),
        "b_w_uq": w((n_b, B_Q_RANK, B_HEADS * B_KV_RANK + IDX_HEADS * IDX_DIM), B_Q_RANK),
        "b_q_norm": gain((n_b, B_KV_RANK)),
        "b_w_uv": w((n_b, B_HEADS, B_KV_RANK, B_VD), B_KV_RANK),
        "c_w_in": w((n_c, D_MODEL, (C_Q_HEADS + 2 * C_KV_HEADS) * C_HD), D_MODEL),
        "c_qk_norm": gain((n_c, 2, C_HD)),
        "c_sinks": rnd((n_c, C_Q_HEADS), 0.5),
    }


def reference(x, meta_tokens, rel_bias, ln_ffn1, ffn1_wi, ffn1_wo, ln_mix, w_out, ln_ffn2, ffn2_wi,
              ffn2_wo, a_w_in, a_qk_norm, a_lambda, a_subln, b_w_in, b_latent_norm, b_w_uq, b_q_norm,
              b_w_uv, c_w_in, c_qk_norm, c_sinks):
    bsz, s, d = x.shape
    k_sel = min(TOPK_MAX, s // 4)
    t = N_META + s
    pos = jnp.arange(t)
    cid = _chunk_id(pos)
    h = jnp.concatenate([jnp.broadcast_to(meta_tokens.astype(x.dtype), (bsz, N_META, d)), x], axis=1)
    for layer in range(DEPTH):
        h = h + 0.5 * _swiglu(_rms_norm(h, ln_ffn1[layer]), ffn1_wi[layer], ffn1_wo[layer])
        hn = _rms_norm(h, ln_mix[layer])
        kind, j = layer % N_MIXERS, layer // N_MIXERS
        if kind == 0:
            lambda_init = 0.8 - 0.6 * math.exp(-0.3 * layer)
            mix = _diff_attention(hn, pos, cid, a_w_in[j], a_qk_norm[j], a_lambda[j], a_subln[j],
                                  lambda_init, rel_bias)
        elif kind == 1:
            mix = _dsa_attention(hn, pos, cid, b_w_in[j], b_latent_norm[j], b_w_uq[j], b_q_norm[j],
                                 b_w_uv[j], rel_bias, k_sel)
        else:
            mix = _swa_attention(hn, c_w_in[j], c_qk_norm[j], c_sinks[j], rel_bias)
        h = h + mix @ w_out[layer]
        h = h + 0.5 * _swiglu(_rms_norm(h, ln_ffn2[layer]), ffn2_wi[layer], ffn2_wo[layer])
    return h[:, N_META:]
```

```python
import math
from contextlib import ExitStack

import numpy as np
import concourse.bass as bass
import concourse.mybir as mybir
from concourse.bass_utils import run_bass_kernel_spmd

F32 = mybir.dt.float32
BF16 = mybir.dt.bfloat16
AF = mybir.ActivationFunctionType
ALU = mybir.AluOpType

D = 1024
SEQ = 2048
NMETA = 16
TT = SEQ + NMETA
DFF = 2816
NFC = DFF // 128
DEPTH = 4
EPS = 1e-6
NEG = -1e30
DMAX = 511
LT = 1151
TILES = [(0, 16)] + [(16 + 512 * j, 16 + 512 * (j + 1)) for j in range(4)]
KTILES = [(0, 16)] + [(16 + 128 * i, 16 + 128 * (i + 1)) for i in range(16)]


class Prog:
    ENGS = ("sp", "act", "pe", "dve", "pool")
    KDMA = 12

    def __init__(self):
        self.ops = []
        self.last_w = {}
        self.readers = {}
        self.fence_idx = None
        self.last_of = {}
        self.dma_hist = {}

    def op(self, eng, fn, r=(), w=(), dma=False, nofence=False):
        idx = len(self.ops)
        deps = set()
        for k in r:
            if k in self.last_w:
                deps.add(self.last_w[k])
        for k in w:
            if k in self.last_w:
                deps.add(self.last_w[k])
            for x in self.readers.get(k, ()):
                deps.add(x)
        for k in r:
            self.readers.setdefault(k, []).append(idx)
        for k in w:
            self.last_w[k] = idx
            self.readers[k] = []
        if self.fence_idx is not None and not nofence:
            deps.add(self.fence_idx)
        if dma:
            hist = self.dma_hist.setdefault(eng, [])
            if len(hist) >= self.KDMA:
                deps.add(hist[-self.KDMA])
            hist.append(idx)
        else:
            self.last_of[eng] = idx
        deps.discard(idx)
        self.ops.append(dict(eng=eng, fn=fn, dma=dma, deps=deps, sig=False))
        return idx

    def fence(self, fn):
        idx = len(self.ops)
        deps = set(self.last_of.values())
        for hist in self.dma_hist.values():
            deps.update(hist[-self.KDMA:])
        self.ops.append(dict(eng="dve", fn=fn, dma=False, deps=deps, sig=False, fence=True))
        self.last_of["dve"] = idx
        self.fence_idx = idx
        self.last_w = {}
        self.readers = {}
        return idx

    def emit(self, nc, final_wait_ops=()):
        ops = self.ops

        def skip(od, o):
            return od["eng"] == "pe" and o["eng"] == "pe" and not od["dma"] and not o["dma"]

        for o in ops:
            for d in o["deps"]:
                if not skip(ops[d], o):
                    ops[d]["sig"] = True
        for d in final_wait_ops:
            ops[d]["sig"] = True
        cnt = {}
        ndma = {}
        epoch = 0
        for o in ops:
            if o.get("fence") and max([v for (e_, k_), v in cnt.items() if k_ == -epoch] + [0]) > 8000:
                epoch += 1
            if o["dma"]:
                n = ndma.get(o["eng"], 0)
                ndma[o["eng"]] = n + 1
                o["sig"] = True
                o["sc"] = (o["eng"], 1 + n % self.KDMA)
                o["val"] = 16 * (n // self.KDMA + 1)
                cnt[o["sc"]] = o["val"]
            else:
                sc = (o["eng"], -epoch)
                o["sc"] = sc
                if o["sig"]:
                    cnt[sc] = cnt.get(sc, 0) + 1
                    o["val"] = cnt[sc]
        self.max_vals = dict(cnt)
        with ExitStack() as es:
            sems = {sc: es.enter_context(nc.semaphore("s_%s_%s" % (sc[0], str(sc[1]).replace("-", "e")))) for sc in sorted(cnt)}
            block = es.enter_context(nc.Block())
            per_eng = {e: [] for e in self.ENGS}
            for i, o in enumerate(ops):
                per_eng[o["eng"]].append(i)

            def make(engname):
                def body(e):
                    waited = {}
                    for i in per_eng[engname]:
                        o = ops[i]
                        need = {}
                        for d in o["deps"]:
                            od = ops[d]
                            if not od["sig"] or skip(od, o):
                                continue
                            sc = od["sc"]
                            if od["val"] > need.get(sc, 0):
                                need[sc] = od["val"]
                        for sc, v in need.items():
                            if waited.get(sc, 0) >= v:
                                continue
                            e.wait_ge(sems[sc], v)
                            waited[sc] = v
                        ins = o["fn"](e)
                        if o["sig"]:
                            ins.then_inc(sems[o["sc"]], 16 if o["dma"] else 1)
                    if engname == "sp":
                        for d in final_wait_ops:
                            od = ops[d]
                            e.wait_ge(sems[od["sc"]], od["val"])
                return body

            block.sync(make("sp"))
            block.scalar(make("act"))
            block.tensor(make("pe"))
            block.vector(make("dve"))
            block.gpsimd(make("pool"))


def _host_tables():
    import jax
    import jax.numpy as jnp
    cpu = jax.devices("cpu")[0]
    with jax.default_device(cpu):
        rel = jnp.arange(LT)
        rel = DMAX - rel
        half, max_exact = 16, 8
        n = jnp.abs(rel)
        large = max_exact + (jnp.log(jnp.maximum(n, 1).astype(jnp.float32) / max_exact)
                             / math.log(128 / max_exact) * (half - max_exact)).astype(jnp.int32)
        large = jnp.minimum(large, half - 1)
        bucket = np.asarray(jnp.where(rel > 0, half, 0) + jnp.where(n < max_exact, n, large))
    E = np.zeros((32, LT), np.float32)
    E[bucket, np.arange(LT)] = 1.0
    kk = np.arange(128)[:, None, None]
    s = np.arange(5)[None, :, None]
    qq = np.arange(512)[None, None, :]
    dc = 2 * (s - 1) + kk // 64 - qq // 64
    maskD = np.where(dc <= 0, 0.0, NEG).astype(np.float32)
    maskW = np.where((dc <= 0) & (dc >= -2), 0.0, NEG).astype(np.float32)
    q1 = np.arange(128)[:, None]
    k1 = np.arange(128)[None, :]
    cm = np.where(k1 // 64 <= q1 // 64, 0.0, NEG).astype(np.float32)
    return dict(tabE=E, maskD=np.ascontiguousarray(maskD), maskW=np.ascontiguousarray(maskW), cm128=cm)


IN_SPECS = [
    ("x", [SEQ, D]), ("meta_tokens", [NMETA, D]), ("rel_bias", [32, 16]),
    ("ln_ffn1", [DEPTH, D]), ("ffn1_wi", [DEPTH, D, 2 * DFF]), ("ffn1_wo", [DEPTH, DFF, D]),
    ("ln_mix", [DEPTH, D]), ("w_out", [DEPTH, D, D]), ("ln_ffn2", [DEPTH, D]),
    ("ffn2_wi", [DEPTH, D, 2 * DFF]), ("ffn2_wo", [DEPTH, DFF, D]),
    ("a_w_in", [2, D, 3072]), ("a_qk_norm", [2, 2, 64]), ("a_lambda", [2, 4, 64]), ("a_subln", [2, 128]),
    ("b_w_in", [1, D, 584]), ("b_latent_norm", [1, 2, 256]), ("b_w_uq", [1, 256, 4608]),
    ("b_q_norm", [1, 256]), ("b_w_uv", [1, 16, 256, 64]),
    ("c_w_in", [1, D, 1280]), ("c_qk_norm", [1, 2, 64]), ("c_sinks", [1, 16]),
    ("tabE", [32, LT]), ("maskD", [128, 5, 512]), ("maskW", [128, 5, 512]), ("cm128", [128, 128]),
]


class Builder:
    def __init__(self, stages=None, dbg=False):
        self.stages = stages
        self.nc = nc = bass.Bass("TRN2", target_bir_lowering=False)
        self.P = Prog()
        self.dh = {}
        for name, shape in IN_SPECS:
            self.dh[name] = nc.dram_tensor(name, shape, F32, kind="ExternalInput")
        self.out_h = nc.dram_tensor("out", [SEQ, D], F32, kind="ExternalOutput")
        self.R = nc.dram_tensor("toep", [16, 128, LT], F32, kind="Internal")
        self.off = 20480
        self.es = ExitStack()
        self.uid = 0

    def sb(self, name, shape, dt):
        n = int(np.prod(shape[1:])) * (4 if dt == F32 else 2)
        n = (n + 31) // 32 * 32
        self.uid += 1
        t = self.nc.alloc_sbuf_tensor_at("%s_%d" % (name, self.uid), list(shape), dt, offset=self.off)
        self.off += n
        assert self.off <= 229376, (name, self.off)
        return t

    def mm(self, out, lhsT, rhs, start, stop, r, w):
        self.P.op("pe", lambda e: e.matmul(out, lhsT, rhs, start=start, stop=stop), r=r, w=w)

    def tr(self, out, in_, ident, r, w):
        self.P.op("pe", lambda e: e.transpose(out, in_, ident), r=r, w=w)

    def act(self, out, in_, func, r, w, bias=None, scale=None):
        kw = {}
        if bias is not None:
            kw["bias"] = bias
        if scale is not None:
            kw["scale"] = scale
        self.P.op("act", lambda e: e.activation(out=out, in_=in_, func=func, **kw), r=r, w=w)

    def tt(self, out, in0, in1, op, r, w, eng="dve"):
        self.P.op(eng, lambda e: e.tensor_tensor(out=out, in0=in0, in1=in1, op=op), r=r, w=w)

    def ts(self, out, in0, s1, s2, op0, r, w, op1=None, eng="dve"):
        if op1 is None:
            self.P.op(eng, lambda e: e.tensor_scalar(out, in0, s1, s2, op0=op0), r=r, w=w)
        else:
            self.P.op(eng, lambda e: e.tensor_scalar(out, in0, s1, s2, op0=op0, op1=op1), r=r, w=w)

    def stt(self, out, in0, scalar, in1, op0, op1, r, w, eng="dve"):
        self.P.op(eng, lambda e: e.scalar_tensor_tensor(out=out, in0=in0, scalar=scalar, in1=in1, op0=op0, op1=op1),
                  r=r, w=w)

    def cp(self, out, in_, r, w, eng="dve"):
        if eng == "act":
            self.P.op("act", lambda e: e.copy(out, in_), r=r, w=w)
        else:
            self.P.op(eng, lambda e: e.tensor_copy(out, in_), r=r, w=w)

    def recip(self, out, in_, r, w):
        self.P.op("dve", lambda e: e.reciprocal(out, in_), r=r, w=w)

    def memset(self, ap, v, w, eng="pool"):
        self.P.op(eng, lambda e: e.memset(ap, v), w=w)

    def dma(self, out, in_, r, w, q="sp", slow=False):
        if slow:
            return self.P.op(q, lambda e: e.dma_start(out=out, in_=in_, allow_slow_non_contiguous=True), r=r, w=w, dma=True)
        return self.P.op(q, lambda e: e.dma_start(out=out, in_=in_), r=r, w=w, dma=True)

    def fence(self):
        fs = self.fscr
        self.P.fence(lambda e: e.memset(fs[:], 0.0))

    def setup(self):
        nc = self.nc
        self.hT = self.sb("hT", [128, 8, TT], F32)
        self.identf = self.sb("identf", [128, 128], F32)
        self.identb = self.sb("identb", [128, 128], BF16)
        self.onesf = self.sb("onesf", [128, 128], F32)
        self.ones1 = self.sb("ones1", [128, 128], BF16)
        self.ones1024 = self.sb("ones1024", [128, 128], BF16)
        self.ones256 = self.sb("ones256", [128, 128], BF16)
        self.ones128 = self.sb("ones128", [128, 128], BF16)
        self.ones64b = self.sb("ones64b", [128, 128], BF16)
        self.epsc = self.sb("epsc", [128, 1], F32)
        self.fscr = self.sb("fscr", [128, 1], F32)
        self.gains = self.sb("gains", [128, 96], F32)
        self.cb = self.sb("cb", [128, 16], F32)
        self.sq = self.sb("sq", [128, 8, 512], BF16)
        self.rs = self.sb("rs", [128, 2, 512], F32)
        self.arena0 = self.off
        self.ps = [self.es.enter_context(nc.psum_tensor("ps%d" % i, [128, 512], F32)) for i in range(7)]
        self.psb = self.es.enter_context(nc.psum_tensor("psb", [128, 1024], BF16))
        self.rs_i = 0

        m = self.memset
        m(self.identf[:], 0.0, ["identf"])
        idf = self.identf
        self.P.op("pool", lambda e: e.affine_select(out=idf[:], in_=idf[:], pattern=[[-1, 128]],
                                                    compare_op=ALU.not_equal, fill=1.0, base=0,
                                                    channel_multiplier=1), r=["identf"], w=["identf"])
        self.cp(self.identb[:], self.identf[:], ["identf"], ["identb"])
        m(self.onesf[:], 1.0, ["onesf"])
        m(self.ones1[:], 1.0, ["ones1"])
        m(self.ones1024[:], 1.0 / 1024, ["ones1024"])
        m(self.ones256[:], 1.0 / 256, ["ones256"])
        m(self.ones128[:], 1.0 / 128, ["ones128"])
        m(self.ones64b[:], 0.0, ["ones64b"])
        m(self.ones64b[0:64, 0:64], 1.0 / 64, ["ones64b"])
        m(self.ones64b[64:128, 64:128], 1.0 / 64, ["ones64b"])
        m(self.epsc[:], EPS, ["epsc"])
        m(self.fscr[:], 0.0, ["fscr"])

        a0 = self.off
        gtmp = self.sb("gtmp", [96, 128], F32)
        for si, nm in enumerate(("ln_ffn1", "ln_mix", "ln_ffn2")):
            src = self.dh[nm].ap().rearrange("l (c p) -> (l c) p", p=128)
            self.dma(gtmp[32 * si:32 * si + 32, :], src, [], ["gtmp%d" % si])
        self.tr(self.ps[5][:, 0:96], gtmp[:, :], self.identf[0:96, 0:96], ["gtmp0", "gtmp1", "gtmp2", "identf"], ["ps5"])
        self.cp(self.gains[:], self.ps[5][:, 0:96], ["ps5"], ["gains"])
        src = bass.AP(tensor=self.dh["rel_bias"], offset=15 * 16, ap=[[0, 128], [1, 16]])
        self.dma(self.cb[:], src, [], ["cb"])
        relb = self.sb("relb", [32, 16], F32)
        relbc = self.sb("relbc", [32, 16, 128], F32)
        tabE = self.sb("tabE", [32, LT], F32)
        trep = self.sb("trep", [128, 2, LT], F32)
        self.dma(relb[:], self.dh["rel_bias"].ap(), [], ["relb"])
        self.dma(tabE[:], self.dh["tabE"].ap(), [], ["tabE"])
        for h in range(16):
            self.cp(relbc[:, h, :], relb[:, h:h + 1].to_broadcast([32, 128]), ["relb"], ["relbc%d" % h])
        self.toep_ready = []
        for h in range(16):
            tb = h % 2
            for ci, (a, b) in enumerate(((0, 512), (512, 1024), (1024, LT))):
                bank = 5 + (ci % 2)
                self.mm(self.ps[bank][:, 0:b - a], relbc[:, h, :], tabE[:, a:b], True, True,
                        ["relbc%d" % h, "tabE"], ["ps%d" % bank])
                self.cp(trep[:, tb, a:b], self.ps[bank][:, 0:b - a], ["ps%d" % bank], ["trep%d" % tb],
                        eng="act" if ci % 2 else "dve")
            self.dma(self.R.ap()[h], trep[:, tb, :], ["trep%d" % tb], ["R%d" % h])
        self.off = a0

    def gain(self, si, layer):
        o = si * 32 + layer * 8
        return self.gains[:, o:o + 8]

    def load_x(self):
        a0 = self.off
        xtok = self.sb("xtok", [128, 2, D], F32)
        mtok = self.sb("mtok", [16, D], F32)
        x = self.dh["x"].ap()
        self.dma(mtok[:], self.dh["meta_tokens"].ap(), [], ["mtok"])
        for c in range(8):
            self.tr(self.ps[4][:, c * 16:(c + 1) * 16], mtok[0:16, c * 128:(c + 1) * 128], self.identf[0:16, 0:16],
                    ["mtok", "identf"], ["ps4"])
        self.cp(self.hT[:, :, 0:16], self.ps[4][:, 0:128].rearrange("p (c t) -> p c t", c=8), ["ps4"],
                [("h", 0, c) for c in range(8)])
        for i in range(16):
            b = i % 2
            self.dma(xtok[:, b, :], x[128 * i:128 * (i + 1), :], [], ["xtok%d" % b])
            col = 16 + 128 * i
            tile = 1 + i // 4
            for half in range(2):
                bank = 5 + half
                for cc in range(4):
                    c = 4 * half + cc
                    self.tr(self.ps[bank][:, cc * 128:(cc + 1) * 128], xtok[:, b, c * 128:(c + 1) * 128], self.identf[:],
                            ["xtok%d" % b, "identf"], ["ps%d" % bank])
                self.cp(self.hT[:, 4 * half:4 * half + 4, col:col + 128],
                        self.ps[bank][:, :].rearrange("p (c t) -> p c t", c=4), ["ps%d" % bank],
                        [("h", tile, 4 * half + cc) for cc in range(4)], eng="act" if half else "dve")
        self.off = a0

    def store_out(self):
        a0 = self.off
        otok = self.sb("otok", [128, 2, D], F32)
        out = self.out_h.ap()
        fin = []
        for i in range(16):
            b = i % 2
            col = 16 + 128 * i
            tile = 1 + i // 4
            for half in range(2):
                bank = 5 + half
                for cc in range(4):
                    c = 4 * half + cc
                    self.tr(self.ps[bank][:, cc * 128:(cc + 1) * 128], self.hT[:, c, col:col + 128], self.identf[:],
                            [("h", tile, c), "identf"], ["ps%d" % bank])
                self.cp(otok[:, b, half * 512:(half + 1) * 512], self.ps[bank][:, :], ["ps%d" % bank],
                        ["otok%d_%d" % (b, half)], eng="act" if half else "dve")
            fin.append(self.dma(out[128 * i:128 * (i + 1), :], otok[:, b, :], ["otok%d_0" % b, "otok%d_1" % b], []))
        self.off = a0
        return fin

    def rstd_from_ps(self, psN, pkey, n, part=128):
        i = self.rs_i
        self.rs_i = (i + 1) % 2
        rsv = self.rs[0:part, i, 0:n]
        key = "rs%d" % i
        self.act(rsv, psN, AF.Sqrt, [pkey, "epsc"], [key], bias=self.epsc[0:part, 0:1])
        self.recip(rsv, rsv, [key], [key])
        return rsv, key

    def rmsnorm(self, gain, xn):
        for ti, (a, b) in enumerate(TILES):
            n = b - a
            hk = [("h", ti, c) for c in range(8)]
            self.act(self.sq[:, :, 0:n], self.hT[:, :, a:b], AF.Square, hk, [("sq", c) for c in range(8)])
            for c in range(8):
                self.mm(self.ps[6][:, 0:n], self.ones1024[:], self.sq[:, c, 0:n], c == 0, c == 7,
                        [("sq", c), "ones1024"], ["ps6"])
            rsv, key = self.rstd_from_ps(self.ps[6][:, 0:n], "ps6", n)
            for c in range(8):
                self.stt(xn[:, c, a:b], self.hT[:, c, a:b], gain[:, c:c + 1], rsv, ALU.mult, ALU.mult,
                         [("h", ti, c), key, "gains"], [("xn", ti, c)])

    def ffn(self, layer, which):
        self.fence()
        a0 = self.off = self.arena0
        xn = self.sb("xn", [128, 8, TT], BF16)
        gT = self.sb("gT", [128, NFC, 1040], BF16)
        wa = self.sb("wa", [128, 2, 8, 256], BF16)
        wb = self.sb("wb", [128, 2, 8, 256], BF16)
        wo = self.sb("wo", [128, 2, NFC, 256], BF16)
        sa = self.sb("sa", [128, 2, 512], BF16)
        wi_d = self.dh["ffn%d_wi" % which].ap()[layer].rearrange("(c p) f -> p c f", p=128)
        wo_d = self.dh["ffn%d_wo" % which].ap()[layer].rearrange("(c p) f -> p c f", p=128)
        gain = self.gain(0 if which == 1 else 2, layer)
        self.rmsnorm(gain, xn)
        cnt = 0
        wcnt = 0
        for grp in ([0, 1, 2], [3, 4]):
            g0 = TILES[grp[0]][0]
            for fg in range(NFC // 2):
                wbuf = wcnt % 2
                wcnt += 1
                self.dma(wa[:, wbuf], wi_d[:, :, fg * 256:(fg + 1) * 256], [], ["wa%d" % wbuf], q="pool")
                self.dma(wb[:, wbuf], wi_d[:, :, DFF + fg * 256:DFF + (fg + 1) * 256], [], ["wb%d" % wbuf], q="pool")
                for fs in range(2):
                    fc = 2 * fg + fs
                    for ti in grp:
                        a, b = TILES[ti]
                        n = b - a
                        x = cnt % 2
                        cnt += 1
                        pA, pB = self.ps[x], self.ps[2 + x]
                        for c in range(8):
                            self.mm(pA[:, 0:n], wa[:, wbuf, c, fs * 128:(fs + 1) * 128], xn[:, c, a:b], c == 0, c == 7,
                                    ["wa%d" % wbuf, ("xn", ti, c)], ["ps%d" % x])
                        for c in range(8):
                            self.mm(pB[:, 0:n], wb[:, wbuf, c, fs * 128:(fs + 1) * 128], xn[:, c, a:b], c == 0, c == 7,
                                    ["wb%d" % wbuf, ("xn", ti, c)], ["ps%d" % (2 + x)])
                        self.act(sa[:, x, 0:n], pA[:, 0:n], AF.Silu, ["ps%d" % x], ["sa%d" % x])
                        self.tt(gT[:, fc, a - g0:b - g0], sa[:, x, 0:n], pB[:, 0:n], ALU.mult,
                                ["sa%d" % x, "ps%d" % (2 + x)], [("g", fc, ti)])
            ocnt = 0
            for dg in range(4):
                obuf = dg % 2
                self.dma(wo[:, obuf], wo_d[:, :, dg * 256:(dg + 1) * 256], [], ["wo%d" % obuf], q="pool")
                for ds_ in range(2):
                    dc = 2 * dg + ds_
                    for ti in grp:
                        a, b = TILES[ti]
                        n = b - a
                        bank = 4 + ocnt % 2
                        ocnt += 1
                        pO = self.ps[bank]
                        for fc in range(NFC):
                            self.mm(pO[:, 0:n], wo[:, obuf, fc, ds_ * 128:(ds_ + 1) * 128], gT[:, fc, a - g0:b - g0],
                                    fc == 0, fc == NFC - 1, ["wo%d" % obuf, ("g", fc, ti)], ["ps%d" % bank])
                        self.stt(self.hT[:, dc, a:b], pO[:, 0:n], 0.5, self.hT[:, dc, a:b], ALU.mult, ALU.add,
                                 ["ps%d" % bank, ("h", ti, dc)], [("h", ti, dc)])
        self.off = a0

    def group_norm_T(self, psrc, pkey, n, ones, okey, gain_ap, dst, dkeys, part=128):
        self.act(self.sq[0:part, 0, 0:n], psrc, AF.Square, [pkey], [("sq", 0)])
        self.mm(self.ps[3][0:part, 0:n], ones[0:part, 0:part], self.sq[0:part, 0, 0:n], True, True, [("sq", 0), okey], ["ps3"])
        rsv, key = self.rstd_from_ps(self.ps[3][0:part, 0:n], "ps3", n, part)
        if gain_ap is None:
            self.tt(dst, psrc, rsv, ALU.mult, [pkey, key], dkeys)
        else:
            self.stt(dst, psrc, gain_ap, rsv, ALU.mult, ALU.mult, [pkey, key, "gsm"], dkeys)

    def load_strip(self, strip, skey, mstrip, mkey, col, mask, maskkey):
        for s in range(5):
            src = bass.AP(tensor=self.R, offset=col * 128 * LT + 639 - 128 * s, ap=[[LT - 1, 128], [1, 512]])
            self.dma(strip[:, s, :], src, ["R%d" % col], [skey + "_%d" % s])
        src = bass.AP(tensor=self.R, offset=col * 128 * LT + 511, ap=[[LT - 1, 16], [1, 528]])
        self.dma(mstrip[0:16, :], src, ["R%d" % col], [mkey])
        if mask is not None:
            self.tt(strip[:, :, :], strip[:, :, :], mask[:, :, :], ALU.add,
                    [skey + "_%d" % s for s in range(5)] + [maskkey], [skey + "_%d" % s for s in range(5)], eng="pool")

    def ktlist(self, ti, strip, skey, mstrip, mkey, far=True):
        if ti == 0:
            return [(0, mstrip[0:16, 0:16], mkey)]
        jj = ti - 1
        out = [(0, mstrip[0:16, 16:528] if jj == 0 else None, mkey)]
        for kt in range(0, 4 * jj + 4):
            if kt <= 4 * jj - 2:
                if far:
                    out.append((1 + kt, None, None))
            else:
                s = kt - (4 * jj - 1)
                out.append((1 + kt, strip[:, s, :], skey + "_%d" % s))
        return out

    def attn_q(self, ti, kts, s_ops, s_keys, v_ops, v_keys, psO_keys, den_ap, den_key, den_m, scale, cbcol, tmp, PT,
               dyn=None):
        a, b = TILES[ti]
        nq = b - a
        nk_tot = len(kts)
        for idx, (kt, strip, skey) in enumerate(kts):
            ka, kb = KTILES[kt]
            nk = kb - ka
            sb_ = idx % 2
            pS = self.ps[sb_][0:nk, 0:nq]
            pairs = s_ops(kt)
            for i, (l, r_) in enumerate(pairs):
                self.mm(pS, l, r_, i == 0, i == len(pairs) - 1, s_keys(kt), ["ps%d" % sb_])
            pt = PT[0:nk, idx % 3, 0:nq]
            ptk = "PT%d" % (idx % 3)
            dk = None
            if dyn is not None:
                dap, dk = dyn(kt)
            if strip is not None or dyn is not None:
                tb = tmp[0:nk, sb_, 0:nq]
                tk = "tmp%d" % sb_
                if strip is not None:
                    self.stt(tb, pS, scale, strip[0:nk, 0:nq], ALU.mult, ALU.add,
                             ["ps%d" % sb_, skey], [tk])
                    if dyn is not None:
                        self.tt(tb, tb, dap, ALU.add, [tk, dk], [tk])
                    self.act(pt, tb, AF.Exp, [tk], [ptk])
                else:
                    self.stt(tb, pS, scale, dap, ALU.mult, ALU.add, ["ps%d" % sb_, dk], [tk])
                    self.act(pt, tb, AF.Exp, [tk, "cb"], [ptk], bias=self.cb[0:nk, cbcol:cbcol + 1])
            else:
                self.act(pt, pS, AF.Exp, ["ps%d" % sb_, "cb"], [ptk], bias=self.cb[0:nk, cbcol:cbcol + 1], scale=scale)
            for (l, o), okey in zip(v_ops(kt), psO_keys):
                self.mm(o, l, pt, idx == 0, idx == nk_tot - 1, [ptk] + v_keys(kt), [okey])
            self.mm(den_ap, self.ones1[0:nk, 0:den_m], pt, idx == 0, idx == nk_tot - 1, [ptk, "ones1"], [den_key])

    def wout_acc(self, wo_ap, wokey, OT, otkey, ti, nchunks=1):
        a, b = TILES[ti]
        n = b - a
        for dc in range(8):
            bank = 5 + dc % 2
            for c in range(nchunks):
                l = wo_ap(c, dc)
                r_ = OT(c)
                self.mm(self.ps[bank][:, 0:n], l, r_, c == 0, c == nchunks - 1, [wokey, otkey], ["ps%d" % bank])
            self.tt(self.hT[:, dc, a:b], self.ps[bank][:, 0:n], self.hT[:, dc, a:b], ALU.add,
                    ["ps%d" % bank, ("h", ti, dc)], [("h", ti, dc)])

    def proj_tokmajor(self, xn, wv, wkey, V, ncols, vkey):
        for kt, (ka, kb) in enumerate(KTILES):
            nk = kb - ka
            ti = 0 if kt == 0 else 1 + (kt - 1) // 4
            bank = 5 + kt % 2
            for c in range(8):
                self.mm(self.ps[bank][0:nk, 0:ncols], xn[:, c, ka:kb], wv[:, c, :], c == 0, c == 7,
                        [("xn", ti, c), wkey], ["ps%d" % bank])
            self.cp(V[0:nk, kt, :], self.ps[bank][0:nk, 0:ncols], ["ps%d" % bank], [(vkey, kt)], eng="act")

    def diff_attn(self, layer, j):
        self.fence()
        a0 = self.off = self.arena0
        xn = self.sb("xn", [128, 8, TT], BF16)
        wq = self.sb("wq", [128, 2, 8, 128], BF16)
        wk = self.sb("wk", [128, 2, 8, 128], BF16)
        wv = self.sb("wv", [128, 2, 8, 128], BF16)
        woh = self.sb("woh", [128, 2, D], BF16)
        QT = self.sb("QT", [128, TT], BF16)
        KT = self.sb("KT", [128, TT], BF16)
        V = self.sb("V", [128, 17, 128], BF16)
        strip = self.sb("strip", [128, 2, 5, 512], F32)
        mstrip = self.sb("mstrip", [16, 2, 528], F32)
        maskD = self.sb("maskD", [128, 5, 512], F32)
        o0 = self.sb("o0", [128, TT], F32)
        tmp = self.sb("tmp", [128, 2, 512], F32)
        PT = self.sb("PT", [128, 3, 512], BF16)
        rden = self.sb("rden", [128, 512], F32)
        ob = self.sb("ob", [128, 2, 512], F32)
        OT = self.sb("OT", [128, 2, 512], BF16)
        gsm = self.sb("gsm", [128, 4], F32)
        lam4 = self.sb("lam4", [64, 4], F32)
        lsc = self.sb("lsc", [128, 4], F32)
        lambda_init = 0.8 - 0.6 * math.exp(-0.3 * layer)
        scale = 64 ** -0.5

        self.rmsnorm(self.gain(1, layer), xn)
        qkn = self.dh["a_qk_norm"].ap()[j]
        for half in range(2):
            for qi in range(2):
                self.dma(gsm[64 * half:64 * half + 64, qi:qi + 1], qkn[qi].rearrange("(p o) -> p o", o=1), [], ["gsm"])
        self.dma(gsm[:, 2:3], self.dh["a_subln"].ap()[j].rearrange("(p o) -> p o", o=1), [], ["gsm"])
        self.ts(gsm[:, 2:3], gsm[:, 2:3], 1.0 - lambda_init, None, ALU.mult, ["gsm"], ["gsm"])
        self.dma(lam4[:], self.dh["a_lambda"].ap()[j].rearrange("f p -> p f"), [], ["lam4"], slow=True)
        self.tt(lsc[0:64, 0:1], lam4[:, 0:1], lam4[:, 1:2], ALU.mult, ["lam4"], ["lsc"])
        self.tt(lsc[0:64, 1:2], lam4[:, 2:3], lam4[:, 3:4], ALU.mult, ["lam4", "lsc"], ["lsc"])
        self.mm(self.ps[3][:, 0:2], self.onesf[0:64, :], lsc[0:64, 0:2], True, True, ["lsc", "onesf"], ["ps3"])
        self.act(lsc[:, 2:4], self.ps[3][:, 0:2], AF.Exp, ["ps3", "lsc"], ["lsc"])
        self.tt(gsm[:, 3:4], lsc[:, 3:4], lsc[:, 2:3], ALU.subtract, ["lsc", "gsm"], ["gsm"])
        self.ts(gsm[:, 3:4], gsm[:, 3:4], -lambda_init, None, ALU.add, ["gsm"], ["gsm"])
        self.dma(maskD[:], self.dh["maskD"].ap(), [], ["maskD"])
        w_in = self.dh["a_w_in"].ap()[j].rearrange("(c p) f -> p c f", p=128)
        w_o = self.dh["w_out"].ap()[layer]
        mapcnt = 0
        for h in range(8):
            wb_ = h % 2
            self.dma(wq[:, wb_], w_in[:, :, h * 128:(h + 1) * 128], [], ["wq%d" % wb_], q="pool")
            self.dma(wk[:, wb_], w_in[:, :, 1024 + h * 128:1024 + (h + 1) * 128], [], ["wk%d" % wb_], q="pool")
            self.dma(wv[:, wb_], w_in[:, :, 2048 + h * 128:2048 + (h + 1) * 128], [], ["wv%d" % wb_], q="pool")
            self.dma(woh[:, wb_, :], w_o[h * 128:(h + 1) * 128, :], [], ["woh%d" % wb_], q="pool")
            for (wt, wkey, dst, dname, gcol) in ((wq, "wq%d" % wb_, QT, "QT", 0), (wk, "wk%d" % wb_, KT, "KT", 1)):
                for ti, (a, b) in enumerate(TILES):
                    n = b - a
                    bank = 5 + ti % 2
                    for c in range(8):
                        self.mm(self.ps[bank][:, 0:n], wt[:, wb_, c, :], xn[:, c, a:b], c == 0, c == 7,
                                [wkey, ("xn", ti, c)], ["ps%d" % bank])
                    self.group_norm_T(self.ps[bank][:, 0:n], "ps%d" % bank, n, self.ones64b, "ones64b",
                                      gsm[:, gcol:gcol + 1], dst[:, a:b], [(dname, ti)])
            self.proj_tokmajor(xn, wv[:, wb_], "wv%d" % wb_, V, 128, "V")
            for m in range(2):
                sbuf_i = mapcnt % 2
                mapcnt += 1
                col = m * 8 + h
                skey, mkey = "strip%d" % sbuf_i, "mstrip%d" % sbuf_i
                self.load_strip(strip[:, sbuf_i], skey, mstrip[:, sbuf_i], mkey, col, maskD, "maskD")
                for ti, (a, b) in enumerate(TILES):
                    nq = b - a
                    kts = self.ktlist(ti, strip[:, sbuf_i], skey, mstrip[:, sbuf_i], mkey)

                    def s_ops(kt, m=m, a=a, b=b):
                        ka, kb = KTILES[kt]
                        return [(KT[64 * m:64 * m + 64, ka:kb], QT[64 * m:64 * m + 64, a:b])]

                    def s_keys(kt, ti=ti):
                        kti = 0 if kt == 0 else 1 + (kt - 1) // 4
                        return [("KT", kti), ("QT", ti)]

                    def v_ops(kt, nq=nq):
                        ka, kb = KTILES[kt]
                        return [(V[0:kb - ka, kt, :], self.ps[2][:, 0:nq])]

                    def v_keys(kt):
                        return [("V", kt)]

                    self.attn_q(ti, kts, s_ops, s_keys, v_ops, v_keys, ["ps2"], self.ps[4][:, 0:nq], "ps4", 128,
                                scale, col, tmp, PT)
                    self.recip(rden[:, 0:nq], self.ps[4][:, 0:nq], ["ps4"], ["rden"])
                    if m == 0:
                        self.tt(o0[:, a:b], self.ps[2][:, 0:nq], rden[:, 0:nq], ALU.mult, ["ps2", "rden"], [("o0", ti)])
                    else:
                        x = ti % 2
                        self.tt(ob[:, x, 0:nq], self.ps[2][:, 0:nq], rden[:, 0:nq], ALU.mult, ["ps2", "rden"], ["ob%d" % x])
                        self.stt(ob[:, x, 0:nq], ob[:, x, 0:nq], gsm[:, 3:4], o0[:, a:b], ALU.mult, ALU.add,
                                 ["ob%d" % x, ("o0", ti), "gsm"], ["ob%d" % x])
                        self.group_norm_T(ob[:, x, 0:nq], "ob%d" % x, nq, self.ones128, "ones128", gsm[:, 2:3],
                                          OT[:, x, 0:nq], ["OT%d" % x])
                        self.wout_acc(lambda c, dc, wb_=wb_: woh[:, wb_, dc * 128:(dc + 1) * 128], "woh%d" % wb_,
                                      lambda c, x=x, nq=nq: OT[:, x, 0:nq], "OT%d" % x, ti)
        self.off = a0

    def swa_attn(self, layer):
        self.fence()
        a0 = self.off = self.arena0
        xn = self.sb("xn", [128, 8, TT], BF16)
        wq = self.sb("wq", [128, 2, 8, 128], BF16)
        wkd = self.sb("wkd", [128, 8, 2, 128], BF16)
        wv = self.sb("wv", [128, 8, 128], BF16)
        woh = self.sb("woh", [128, 2, D], BF16)
        QT = self.sb("QT", [128, TT], BF16)
        KT = self.sb("KT", [128, 2, TT], BF16)
        V = self.sb("V", [128, 17, 128], BF16)
        strip = self.sb("strip", [128, 2, 5, 512], F32)
        mstrip = self.sb("mstrip", [16, 2, 528], F32)
        maskW = self.sb("maskW", [128, 5, 512], F32)
        tmp = self.sb("tmp", [128, 2, 512], F32)
        PT = self.sb("PT", [128, 3, 512], BF16)
        rden = self.sb("rden", [128, 512], F32)
        OT = self.sb("OT", [128, 2, 512], BF16)
        gsm = self.sb("gsm", [128, 4], F32)
        esk = self.sb("esk", [128, 8], F32)
        scale = 64 ** -0.5

        self.rmsnorm(self.gain(1, layer), xn)
        qkn = self.dh["c_qk_norm"].ap()[0]
        for half in range(2):
            for qi in range(2):
                self.dma(gsm[64 * half:64 * half + 64, qi:qi + 1], qkn[qi].rearrange("(p o) -> p o", o=1), [], ["gsm"])
        for half in range(2):
            src = bass.AP(tensor=self.dh["c_sinks"], offset=half, ap=[[0, 64], [2, 8]])
            self.dma(esk[64 * half:64 * half + 64, :], src, [], ["esk"], slow=True)
        self.act(esk[:], esk[:], AF.Exp, ["esk"], ["esk"])
        self.dma(maskW[:], self.dh["maskW"].ap(), [], ["maskW"])
        w_in = self.dh["c_w_in"].ap()[0].rearrange("(c p) f -> p c f", p=128)
        w_o = self.dh["w_out"].ap()[layer]
        for g in range(2):
            for half in range(2):
                self.dma(wkd[:, :, g, 64 * half:64 * half + 64], w_in[:, :, 1024 + 64 * g:1024 + 64 * g + 64], [],
                         ["wkd"], q="pool")
        self.dma(wv[:], w_in[:, :, 1152:1280], [], ["wv"], q="pool")
        for g in range(2):
            for ti, (a, b) in enumerate(TILES):
                n = b - a
                bank = 5 + ti % 2
                for c in range(8):
                    self.mm(self.ps[bank][:, 0:n], wkd[:, c, g, :], xn[:, c, a:b], c == 0, c == 7,
                            ["wkd", ("xn", ti, c)], ["ps%d" % bank])
                self.group_norm_T(self.ps[bank][:, 0:n], "ps%d" % bank, n, self.ones64b, "ones64b", gsm[:, 1:2],
                                  KT[:, g, a:b], [("KT", g, ti)])
        self.proj_tokmajor(xn, wv, "wv", V, 128, "V")
        for cpair in range(8):
            wb_ = cpair % 2
            g = cpair // 4
            self.dma(wq[:, wb_], w_in[:, :, cpair * 128:(cpair + 1) * 128], [], ["wq%d" % wb_], q="pool")
            self.dma(woh[:, wb_, :], w_o[cpair * 128:(cpair + 1) * 128, :], [], ["woh%d" % wb_], q="pool")
            for ti, (a, b) in enumerate(TILES):
                n = b - a
                bank = 5 + ti % 2
                for c in range(8):
                    self.mm(self.ps[bank][:, 0:n], wq[:, wb_, c, :], xn[:, c, a:b], c == 0, c == 7,
                            ["wq%d" % wb_, ("xn", ti, c)], ["ps%d" % bank])
                self.group_norm_T(self.ps[bank][:, 0:n], "ps%d" % bank, n, self.ones64b, "ones64b", gsm[:, 0:1],
                                  QT[:, a:b], [("QT", ti)])
            for s in range(2):
                self.load_strip(strip[:, s], "strip%d" % s, mstrip[:, s], "mstrip%d" % s, 2 * cpair + s, maskW, "maskW")
            for ti, (a, b) in enumerate(TILES):
                nq = b - a
                for s in range(2):
                    kts = self.ktlist(ti, strip[:, s], "strip%d" % s, mstrip[:, s], "mstrip%d" % s, far=False)

                    def s_ops(kt, s=s, a=a, b=b, g=g):
                        ka, kb = KTILES[kt]
                        return [(KT[64 * s:64 * s + 64, g, ka:kb], QT[64 * s:64 * s + 64, a:b])]

                    def s_keys(kt, ti=ti, g=g):
                        kti = 0 if kt == 0 else 1 + (kt - 1) // 4
                        return [("KT", g, kti), ("QT", ti)]

                    def v_ops(kt, nq=nq, s=s, g=g):
                        ka, kb = KTILES[kt]
                        return [(V[0:kb - ka, kt, 64 * g:64 * g + 64], self.ps[2][64 * s:64 * s + 64, 0:nq])]

                    def v_keys(kt):
                        return [("V", kt)]

                    self.attn_q(ti, kts, s_ops, s_keys, v_ops, v_keys, ["ps2_%d" % s],
                                self.ps[4][64 * s:64 * s + 64, 0:nq], "ps4_%d" % s, 64, scale, 2 * cpair + s, tmp, PT)
                x = ti % 2
                self.ts(rden[:, 0:nq], self.ps[4][:, 0:nq], esk[:, cpair:cpair + 1], None, ALU.add,
                        ["ps4_0", "ps4_1", "esk"], ["rden"])
                self.recip(rden[:, 0:nq], rden[:, 0:nq], ["rden"], ["rden"])
                self.tt(OT[:, x, 0:nq], self.ps[2][:, 0:nq], rden[:, 0:nq], ALU.mult, ["ps2_0", "ps2_1", "rden"],
                        ["OT%d" % x])
                self.wout_acc(lambda c, dc, wb_=wb_: woh[:, wb_, dc * 128:(dc + 1) * 128], "woh%d" % wb_,
                              lambda c, x=x, nq=nq: OT[:, x, 0:nq], "OT%d" % x, ti)
        self.off = a0

    def dsa_attn(self, layer):
        self.fence()
        a0 = self.off = self.arena0
        P = self.P
        xn = self.sb("xn", [128, 8, TT], BF16)
        after_xn = self.off
        early0 = self.off
        w1 = self.sb("w1", [128, 8, 512], BF16)
        wki = self.sb("wki", [128, 8, 128], BF16)
        wwi = self.sb("wwi", [128, 8, 8], BF16)
        early1 = self.off
        self.off = early0
        work = self.sb("work", [128, TT], F32)
        qabs = self.sb("qabs", [128, 2, 512], BF16)
        assert self.off <= early1
        self.off = early1
        wuq = self.sb("wuq", [128, 2, 2, 256], BF16)
        wqi = self.sb("wqi", [128, 2, 512], BF16)
        wuv = self.sb("wuv", [128, 16, 2, 64], BF16)
        wo = self.sb("wo", [128, 2, 8, 128], BF16)
        cqT = self.sb("cqT", [128, 2, TT], BF16)
        ckvT = self.sb("ckvT", [128, 2, TT], BF16)
        ckv = self.sb("ckv", [128, 17, 256], BF16)
        kiT = self.sb("kiT", [128, TT], BF16)
        qiT = self.sb("qiT", [128, 4, 512], BF16)
        wsc = self.sb("wsc", [128, 17, 8], F32)
        strip = self.sb("strip", [128, 5, 512], F32)
        mstrip = self.sb("mstrip", [16, 528], F32)
        tmp = self.sb("tmp", [128, 2, 512], F32)
        rden = self.sb("rden", [128, 512], F32)
        olat = self.sb("olat", [128, 2, 512], BF16)
        oq = self.sb("oq", [128, 8, 512], BF16)
        relu = self.sb("relu", [128, 2, 512], F32)
        cm = self.sb("cm", [128, 128], F32)
        gsm = self.sb("gsm", [128, 8], F32)
        mx = self.sb("mx", [128, 8], F32)
        thr = self.sb("thr", [128, 1], F32)
        end_arena = self.off
        self.off = self.arena0
        maskT = self.sb("maskT", [128, 17, 512], BF16)
        score = self.sb("score", [128, TT], F32)
        m01 = self.sb("m01", [128, TT], BF16)
        PT = self.sb("PT", [128, 3, 512], BF16)
        assert self.off <= after_xn
        self.off = end_arena

        self.rmsnorm(self.gain(1, layer), xn)
        w_in = self.dh["b_w_in"].ap()[0].rearrange("(c p) f -> p c f", p=128)
        self.dma(w1[:], w_in[:, :, 0:512], [], ["w1"], q="pool")
        for half in range(2):
            self.dma(wki[:, :, 64 * half:64 * half + 64], w_in[:, :, 512:576], [], ["wki"], q="pool")
        self.dma(wwi[:], w_in[:, :, 576:584], [], ["wwi"], q="pool")
        ln = self.dh["b_latent_norm"].ap()[0]
        for i in range(2):
            self.dma(gsm[:, 2 * i:2 * i + 2], ln[i].rearrange("(c p) -> p c", p=128), [], ["gsm"], slow=True)
        self.dma(gsm[:, 4:6], self.dh["b_q_norm"].ap()[0].rearrange("(c p) -> p c", p=128), [], ["gsm"], slow=True)
        self.dma(cm[:], self.dh["cm128"].ap(), [], ["cm"])
        uv = self.dh["b_w_uv"].ap()[0]
        for h in range(16):
            self.dma(wuv[:, h], uv[h].rearrange("(c p) e -> p c e", p=128), [], ["wuv"], q="pool")
        w_uq = self.dh["b_w_uq"].ap()[0].rearrange("(c p) f -> p c f", p=128)
        self.dma(wqi[:], w_uq[:, :, 4096:4608], [], ["wqi"], q="pool")
        w_o = self.dh["w_out"].ap()[layer].rearrange("(c p) f -> p c f", p=128)

        for ti, (a, b) in enumerate(TILES):
            n = b - a
            for which, dst, dname in ((0, cqT, "cqT"), (1, ckvT, "ckvT")):
                for rc in range(2):
                    bank = 5 + rc
                    for c in range(8):
                        self.mm(self.ps[bank][:, 0:n], w1[:, c, which * 256 + rc * 128:which * 256 + (rc + 1) * 128],
                                xn[:, c, a:b], c == 0, c == 7, ["w1", ("xn", ti, c)], ["ps%d" % bank])
                    self.act(self.sq[:, rc, 0:n], self.ps[bank][:, 0:n], AF.Square, ["ps%d" % bank], [("sq", rc)])
                for rc in range(2):
                    self.mm(self.ps[3][:, 0:n], self.ones256[:], self.sq[:, rc, 0:n], rc == 0, rc == 1,
                            [("sq", rc), "ones256"], ["ps3"])
                rsv, key = self.rstd_from_ps(self.ps[3][:, 0:n], "ps3", n)
                for rc in range(2):
                    self.stt(dst[:, rc, a:b], self.ps[5 + rc][:, 0:n], gsm[:, 2 * which + rc:2 * which + rc + 1], rsv,
                             ALU.mult, ALU.mult, ["ps%d" % (5 + rc), key, "gsm"], [(dname, rc, ti)])
            bank = 5
            for c in range(8):
                self.mm(self.ps[bank][:, 0:n], wki[:, c, :], xn[:, c, a:b], c == 0, c == 7, ["wki", ("xn", ti, c)],
                        ["ps%d" % bank])
            self.group_norm_T(self.ps[bank][:, 0:n], "ps%d" % bank, n, self.ones64b, "ones64b", None, kiT[:, a:b],
                              [("kiT", ti)])
        for kt, (ka, kb) in enumerate(KTILES):
            nk = kb - ka
            ti = 0 if kt == 0 else 1 + (kt - 1) // 4
            bank = 5 + kt % 2
            for c in range(8):
                self.mm(self.ps[bank][0:nk, 0:8], xn[:, c, ka:kb], wwi[:, c, :], c == 0, c == 7,
                        [("xn", ti, c), "wwi"], ["ps%d" % bank])
            self.ts(wsc[0:nk, kt, :], self.ps[bank][0:nk, 0:8], (8 ** -0.5) * (64 ** -0.5), None, ALU.mult,
                    ["ps%d" % bank], [("wsc", kt)])
            for rc in range(2):
                self.tr(self.psb[0:nk, rc * 128:(rc + 1) * 128], ckvT[:, rc, ka:kb], self.identb[:],
                        [("ckvT", rc, ti), "identb"], ["psb"])
            self.cp(ckv[0:nk, kt, :], self.psb[0:nk, 0:256], ["psb"], [("ckv", kt)], eng="act")
        self.fence()
        scale = 256 ** -0.5
        for ti, (a, b) in enumerate(TILES):
            nq = b - a
            jj = ti - 1
            for ch in range(4):
                bank = 5 + ch % 2
                for rc in range(2):
                    self.mm(self.ps[bank][:, 0:nq], wqi[:, rc, ch * 128:(ch + 1) * 128], cqT[:, rc, a:b], rc == 0, rc == 1,
                            ["wqi", ("cqT", rc, ti)], ["ps%d" % bank])
                self.cp(qiT[:, ch, 0:nq], self.ps[bank][:, 0:nq], ["ps%d" % bank], [("qiT", ch)], eng="act")
            if ti == 0:
                self.memset(maskT[0:16, 0, 0:16], 0.0, [("maskT", 0)])
            else:
                self.memset(maskT[:, 4 * jj + 1:4 * jj + 5, :], NEG, [("maskT", 4 * jj + 1 + u) for u in range(4)])
                for sub in range(4):
                    i = 4 * jj + sub
                    qa = a + 128 * sub
                    wd = 16 + 128 * (i + 1)
                    qkt = 1 + i
                    blocks = [(c0, min(c0 + 512, wd)) for c0 in range(0, wd, 512)]
                    for bi, (c0, c1) in enumerate(blocks):
                        nkc = c1 - c0
                        for hh in range(8):
                            bank = 5 + hh % 2
                            s_ = hh % 2
                            self.mm(self.ps[bank][:, 0:nkc], qiT[64 * s_:64 * s_ + 64, hh // 2, 128 * sub:128 * (sub + 1)],
                                    kiT[64 * s_:64 * s_ + 64, c0:c1], True, True,
                                    [("qiT", hh // 2)] + [("kiT", t) for t in range(5)], ["ps%d" % bank])
                            self.act(relu[:, s_, 0:nkc], self.ps[bank][:, 0:nkc], AF.Relu, ["ps%d" % bank], ["relu%d" % s_])
                            if hh == 0:
                                self.ts(score[:, c0:c1], relu[:, s_, 0:nkc], wsc[:, qkt, hh:hh + 1], None, ALU.mult,
                                        ["relu%d" % s_, ("wsc", qkt)], [("score", bi)])
                            else:
                                self.stt(score[:, c0:c1], relu[:, s_, 0:nkc], wsc[:, qkt, hh:hh + 1], score[:, c0:c1],
                                         ALU.mult, ALU.add, ["relu%d" % s_, ("wsc", qkt), ("score", bi)], [("score", bi)])
                    skeys = [("score", bi) for bi in range(len(blocks))]
                    self.tt(score[:, wd - 128:wd], score[:, wd - 128:wd], cm[:], ALU.add, skeys + ["cm"], skeys)
                    src = score
                    for rnd in range(32):
                        self.P.op("dve", lambda e, src=src, wd=wd: e.max(out=mx[:], in_=src[:, 0:wd]),
                                  r=skeys + ["work"], w=["mx"])
                        if rnd < 31:
                            self.P.op("dve", lambda e, src=src, wd=wd: e.match_replace(
                                out=work[:, 0:wd], in_to_replace=mx[:], in_values=src[:, 0:wd], imm_value=NEG),
                                r=skeys + ["mx", "work"], w=["work"])
                            src = work
                    self.ts(thr[:], mx[:, 7:8], -1e29, None, ALU.max, ["mx"], ["thr"])
                    self.ts(m01[:, 0:wd], score[:, 0:wd], thr[:, 0:1], None, ALU.is_ge, skeys + ["thr"], ["m01"])
                    self.tr(self.psb[0:16, 0:128], m01[:, 0:16], self.identb[:], ["m01", "identb"], ["psb"])
                    self.ts(maskT[0:16, 0, 128 * sub:128 * (sub + 1)], self.psb[0:16, 0:128], -1.0, -NEG, ALU.add,
                            ["psb"], [("maskT", 0)], op1=ALU.mult)
                    for k0 in range(1, i + 2, 8):
                        k1 = min(k0 + 8, i + 2)
                        for kt in range(k0, k1):
                            ka, kb = KTILES[kt]
                            self.tr(self.psb[:, (kt - k0) * 128:(kt - k0 + 1) * 128], m01[:, ka:kb], self.identb[:],
                                    ["m01", "identb"], ["psb"])
                        self.ts(maskT[:, k0:k1, 128 * sub:128 * (sub + 1)],
                                self.psb[:, 0:(k1 - k0) * 128].rearrange("p (k q) -> p k q", q=128), -1.0, -NEG, ALU.add,
                                ["psb"], [("maskT", kt) for kt in range(k0, k1)], op1=ALU.mult)
            for h in range(16):
                ub = h % 2
                self.dma(wuq[:, ub], w_uq[:, :, h * 256:(h + 1) * 256], [], ["wuq%d" % ub], q="pool")
                for rc in range(2):
                    bank = 5 + rc
                    for c2 in range(2):
                        self.mm(self.ps[bank][:, 0:nq], wuq[:, ub, c2, rc * 128:(rc + 1) * 128], cqT[:, c2, a:b],
                                c2 == 0, c2 == 1, ["wuq%d" % ub, ("cqT", c2, ti)], ["ps%d" % bank])
                    self.act(self.sq[:, rc, 0:nq], self.ps[bank][:, 0:nq], AF.Square, ["ps%d" % bank], [("sq", rc)])
                for rc in range(2):
                    self.mm(self.ps[3][:, 0:nq], self.ones256[:], self.sq[:, rc, 0:nq], rc == 0, rc == 1,
                            [("sq", rc), "ones256"], ["ps3"])
                rsv, key = self.rstd_from_ps(self.ps[3][:, 0:nq], "ps3", nq)
                for rc in range(2):
                    self.stt(qabs[:, rc, 0:nq], self.ps[5 + rc][:, 0:nq], gsm[:, 4 + rc:5 + rc], rsv, ALU.mult, ALU.mult,
                             ["ps%d" % (5 + rc), key, "gsm"], [("qabs", rc)])
                self.load_strip(strip, "strip", mstrip, "mstrip", h, None, None)
                kts = self.ktlist(ti, strip, "strip", mstrip, "mstrip")

                def s_ops(kt, nq=nq):
                    ka, kb = KTILES[kt]
                    return [(ckvT[:, rc, ka:kb], qabs[:, rc, 0:nq]) for rc in range(2)]

                def s_keys(kt):
                    kti = 0 if kt == 0 else 1 + (kt - 1) // 4
                    return [("ckvT", 0, kti), ("ckvT", 1, kti), ("qabs", 0), ("qabs", 1)]

                def v_ops(kt, nq=nq):
                    ka, kb = KTILES[kt]
                    return [(ckv[0:kb - ka, kt, rc * 128:(rc + 1) * 128], self.ps[2 + rc][:, 0:nq]) for rc in range(2)]

                def v_keys(kt):
                    return [("ckv", kt)]

                def dyn(kt, nq=nq):
                    ka, kb = KTILES[kt]
                    return maskT[0:kb - ka, kt, 0:nq], ("maskT", kt)

                self.attn_q(ti, kts, s_ops, s_keys, v_ops, v_keys, ["ps2", "ps3"], self.ps[4][:, 0:nq], "ps4", 128,
                            scale, h, tmp, PT, dyn=dyn)
                self.recip(rden[:, 0:nq], self.ps[4][:, 0:nq], ["ps4"], ["rden"])
                for rc in range(2):
                    self.tt(olat[:, rc, 0:nq], self.ps[2 + rc][:, 0:nq], rden[:, 0:nq], ALU.mult,
                            ["ps%d" % (2 + rc), "rden"], [("olat", rc)])
                s_ = h % 2
                bank = 5 + h % 2
                for rc in range(2):
                    self.mm(self.ps[bank][64 * s_:64 * s_ + 64, 0:nq], wuv[:, h, rc, :], olat[:, rc, 0:nq], rc == 0, rc == 1,
                            ["wuv", ("olat", rc)], ["ps%d" % bank])
                self.cp(oq[64 * s_:64 * s_ + 64, h // 2, 0:nq], self.ps[bank][64 * s_:64 * s_ + 64, 0:nq], ["ps%d" % bank],
                        [("oq", h // 2, s_)], eng="act")
            for dc in range(8):
                ob_ = dc % 2
                self.dma(wo[:, ob_], w_o[:, :, dc * 128:(dc + 1) * 128], [], ["wo%d" % ob_], q="pool")
                bank = 5 + dc % 2
                for c in range(8):
                    self.mm(self.ps[bank][:, 0:nq], wo[:, ob_, c, :], oq[:, c, 0:nq], c == 0, c == 7,
                            ["wo%d" % ob_, ("oq", c, 0), ("oq", c, 1)], ["ps%d" % bank])
                self.tt(self.hT[:, dc, a:b], self.ps[bank][:, 0:nq], self.hT[:, dc, a:b], ALU.add,
                        ["ps%d" % bank, ("h", ti, dc)], [("h", ti, dc)])
        self.off = a0

    def build(self):
        self.setup()
        self.load_x()
        st = self.stages
        for layer in range(DEPTH):
            if st is None or ("f1", layer) in st:
                self.ffn(layer, 1)
            if st is None or ("mix", layer) in st:
                kind, j = layer % 3, layer // 3
                if kind == 0:
                    self.diff_attn(layer, j)
                elif kind == 1:
                    self.dsa_attn(layer)
                else:
                    self.swa_attn(layer)
            if st is None or ("f2", layer) in st:
                self.ffn(layer, 2)
        self.fence()
        fin = self.store_out()
        self.P.emit(self.nc, final_wait_ops=fin)
        self.es.close()
        return self.nc


_TABLES = None


def _in_maps(inputs, cores):
    global _TABLES
    if _TABLES is None:
        _TABLES = _host_tables()
    shared = {}
    for name, shape in IN_SPECS:
        if name == "x":
            continue
        if name in _TABLES:
            shared[name] = _TABLES[name]
        else:
            shared[name] = np.ascontiguousarray(np.asarray(inputs[name], dtype=np.float32))
    x = np.asarray(inputs["x"], dtype=np.float32)
    maps = []
    for b in cores:
        m = dict(shared)
        m["x"] = np.ascontiguousarray(x[b])
        maps.append(m)
    return maps


def kernel(**inputs):
    nc = Builder().build()
    maps = _in_maps(inputs, list(range(8)))
    res = run_bass_kernel_spmd(nc, maps, core_ids=list(range(8)))
    return np.stack([np.asarray(r["out"], dtype=np.float32) for r in res.results], axis=0)
```

```python
import math
from contextlib import ExitStack

import numpy as np
import concourse.bass as bass
import concourse.mybir as mybir
from concourse.bass_utils import run_bass_kernel_spmd

F32 = mybir.dt.float32
BF16 = mybir.dt.bfloat16
AF = mybir.ActivationFunctionType
ALU = mybir.AluOpType

D = 1024
SEQ = 2048
NMETA = 16
TT = SEQ + NMETA
DFF = 2816
NFC = DFF // 128
DEPTH = 4
EPS = 1e-6
NEG = -1e30
DMAX = 511
LT = 1151
TILES = [(0, 16)] + [(16 + 512 * j, 16 + 512 * (j + 1)) for j in range(4)]
KTILES = [(0, 16)] + [(16 + 128 * i, 16 + 128 * (i + 1)) for i in range(16)]


class Prog:
    ENGS = ("sp", "act", "pe", "dve", "pool")
    KDMA = 12

    def __init__(self):
        self.ops = []
        self.last_w = {}
        self.readers = {}
        self.fence_idx = None
        self.last_of = {}
        self.dma_hist = {}

    def op(self, eng, fn, r=(), w=(), dma=False, nofence=False):
        idx = len(self.ops)
        deps = set()
        for k in r:
            if k in self.last_w:
                deps.add(self.last_w[k])
        for k in w:
            if k in self.last_w:
                deps.add(self.last_w[k])
            for x in self.readers.get(k, ()):
                deps.add(x)
        for k in r:
            self.readers.setdefault(k, []).append(idx)
        for k in w:
            self.last_w[k] = idx
            self.readers[k] = []
        if self.fence_idx is not None and not nofence:
            deps.add(self.fence_idx)
        if dma:
            hist = self.dma_hist.setdefault(eng, [])
            if len(hist) >= self.KDMA:
                deps.add(hist[-self.KDMA])
            hist.append(idx)
        else:
            self.last_of[eng] = idx
        deps.discard(idx)
        self.ops.append(dict(eng=eng, fn=fn, dma=dma, deps=deps, sig=False))
        return idx

    def fence(self, fn):
        idx = len(self.ops)
        deps = set(self.last_of.values())
        for hist in self.dma_hist.values():
            deps.update(hist[-self.KDMA:])
        self.ops.append(dict(eng="dve", fn=fn, dma=False, deps=deps, sig=False, fence=True))
        self.last_of["dve"] = idx
        self.fence_idx = idx
        self.last_w = {}
        self.readers = {}
        return idx

    def emit(self, nc, final_wait_ops=()):
        ops = self.ops

        def skip(od, o):
            return od["eng"] == "pe" and o["eng"] == "pe" and not od["dma"] and not o["dma"]

        for o in ops:
            for d in o["deps"]:
                if not skip(ops[d], o):
                    ops[d]["sig"] = True
        for d in final_wait_ops:
            ops[d]["sig"] = True
        cnt = {}
        ndma = {}
        epoch = 0
        for o in ops:
            if o.get("fence") and max([v for (e_, k_), v in cnt.items() if k_ == -epoch] + [0]) > 8000:
                epoch += 1
            if o["dma"]:
                n = ndma.get(o["eng"], 0)
                ndma[o["eng"]] = n + 1
                o["sig"] = True
                o["sc"] = (o["eng"], 1 + n % self.KDMA)
                o["val"] = 16 * (n // self.KDMA + 1)
                cnt[o["sc"]] = o["val"]
            else:
                sc = (o["eng"], -epoch)
                o["sc"] = sc
                if o["sig"]:
                    cnt[sc] = cnt.get(sc, 0) + 1
                    o["val"] = cnt[sc]
        self.max_vals = dict(cnt)
        with ExitStack() as es:
            sems = {sc: es.enter_context(nc.semaphore("s_%s_%s" % (sc[0], str(sc[1]).replace("-", "e")))) for sc in sorted(cnt)}
            block = es.enter_context(nc.Block())
            per_eng = {e: [] for e in self.ENGS}
            for i, o in enumerate(ops):
                per_eng[o["eng"]].append(i)

            def make(engname):
                def body(e):
                    waited = {}
                    for i in per_eng[engname]:
                        o = ops[i]
                        need = {}
                        for d in o["deps"]:
                            od = ops[d]
                            if not od["sig"] or skip(od, o):
                                continue
                            sc = od["sc"]
                            if od["val"] > need.get(sc, 0):
                                need[sc] = od["val"]
                        for sc, v in need.items():
                            if waited.get(sc, 0) >= v:
                                continue
                            e.wait_ge(sems[sc], v)
                            waited[sc] = v
                        ins = o["fn"](e)
                        if o["sig"]:
                            ins.then_inc(sems[o["sc"]], 16 if o["dma"] else 1)
                    if engname == "sp":
                        for d in final_wait_ops:
                            od = ops[d]
                            e.wait_ge(sems[od["sc"]], od["val"])
                return body

            block.sync(make("sp"))
            block.scalar(make("act"))
            block.tensor(make("pe"))
            block.vector(make("dve"))
            block.gpsimd(make("pool"))


def _host_tables():
    import jax
    import jax.numpy as jnp
    cpu = jax.devices("cpu")[0]
    with jax.default_device(cpu):
        rel = jnp.arange(LT)
        rel = DMAX - rel
        half, max_exact = 16, 8
        n = jnp.abs(rel)
        large = max_exact + (jnp.log(jnp.maximum(n, 1).astype(jnp.float32) / max_exact)
                             / math.log(128 / max_exact) * (half - max_exact)).astype(jnp.int32)
        large = jnp.minimum(large, half - 1)
        bucket = np.asarray(jnp.where(rel > 0, half, 0) + jnp.where(n < max_exact, n, large))
    E = np.zeros((32, LT), np.float32)
    E[bucket, np.arange(LT)] = 1.0
    kk = np.arange(128)[:, None, None]
    s = np.arange(5)[None, :, None]
    qq = np.arange(512)[None, None, :]
    dc = 2 * (s - 1) + kk // 64 - qq // 64
    maskD = np.where(dc <= 0, 0.0, NEG).astype(np.float32)
    maskW = np.where((dc <= 0) & (dc >= -2), 0.0, NEG).astype(np.float32)
    q1 = np.arange(128)[:, None]
    k1 = np.arange(128)[None, :]
    cm = np.where(k1 // 64 <= q1 // 64, 0.0, NEG).astype(np.float32)
    return dict(tabE=E, maskD=np.ascontiguousarray(maskD), maskW=np.ascontiguousarray(maskW), cm128=cm)


IN_SPECS = [
    ("x", [SEQ, D]), ("meta_tokens", [NMETA, D]), ("rel_bias", [32, 16]),
    ("ln_ffn1", [DEPTH, D]), ("ffn1_wi", [DEPTH, D, 2 * DFF]), ("ffn1_wo", [DEPTH, DFF, D]),
    ("ln_mix", [DEPTH, D]), ("w_out", [DEPTH, D, D]), ("ln_ffn2", [DEPTH, D]),
    ("ffn2_wi", [DEPTH, D, 2 * DFF]), ("ffn2_wo", [DEPTH, DFF, D]),
    ("a_w_in", [2, D, 3072]), ("a_qk_norm", [2, 2, 64]), ("a_lambda", [2, 4, 64]), ("a_subln", [2, 128]),
    ("b_w_in", [1, D, 584]), ("b_latent_norm", [1, 2, 256]), ("b_w_uq", [1, 256, 4608]),
    ("b_q_norm", [1, 256]), ("b_w_uv", [1, 16, 256, 64]),
    ("c_w_in", [1, D, 1280]), ("c_qk_norm", [1, 2, 64]), ("c_sinks", [1, 16]),
    ("tabE", [32, LT]), ("maskD", [128, 5, 512]), ("maskW", [128, 5, 512]), ("cm128", [128, 128]),
]


class Builder:
    def __init__(self, stages=None, dbg=False):
        self.stages = stages
        self.nc = nc = bass.Bass("TRN2", target_bir_lowering=False)
        self.P = Prog()
        self.dh = {}
        for name, shape in IN_SPECS:
            self.dh[name] = nc.dram_tensor(name, shape, F32, kind="ExternalInput")
        self.out_h = nc.dram_tensor("out", [SEQ, D], F32, kind="ExternalOutput")
        self.R = nc.dram_tensor("toep", [16, 128, LT], F32, kind="Internal")
        self.off = 20480
        self.es = ExitStack()
        self.uid = 0

    def sb(self, name, shape, dt):
        n = int(np.prod(shape[1:])) * (4 if dt == F32 else 2)
        n = (n + 31) // 32 * 32
        self.uid += 1
        t = self.nc.alloc_sbuf_tensor_at("%s_%d" % (name, self.uid), list(shape), dt, offset=self.off)
        self.off += n
        assert self.off <= 229376, (name, self.off)
        return t

    def mm(self, out, lhsT, rhs, start, stop, r, w):
        self.P.op("pe", lambda e: e.matmul(out, lhsT, rhs, start=start, stop=stop), r=r, w=w)

    def tr(self, out, in_, ident, r, w):
        self.P.op("pe", lambda e: e.transpose(out, in_, ident), r=r, w=w)

    def act(self, out, in_, func, r, w, bias=None, scale=None):
        kw = {}
        if bias is not None:
            kw["bias"] = bias
        if scale is not None:
            kw["scale"] = scale
        self.P.op("act", lambda e: e.activation(out=out, in_=in_, func=func, **kw), r=r, w=w)

    def tt(self, out, in0, in1, op, r, w, eng="dve"):
        self.P.op(eng, lambda e: e.tensor_tensor(out=out, in0=in0, in1=in1, op=op), r=r, w=w)

    def ts(self, out, in0, s1, s2, op0, r, w, op1=None, eng="dve"):
        if op1 is None:
            self.P.op(eng, lambda e: e.tensor_scalar(out, in0, s1, s2, op0=op0), r=r, w=w)
        else:
            self.P.op(eng, lambda e: e.tensor_scalar(out, in0, s1, s2, op0=op0, op1=op1), r=r, w=w)

    def stt(self, out, in0, scalar, in1, op0, op1, r, w, eng="dve"):
        self.P.op(eng, lambda e: e.scalar_tensor_tensor(out=out, in0=in0, scalar=scalar, in1=in1, op0=op0, op1=op1),
                  r=r, w=w)

    def cp(self, out, in_, r, w, eng="dve"):
        if eng == "act":
            self.P.op("act", lambda e: e.copy(out, in_), r=r, w=w)
        else:
            self.P.op(eng, lambda e: e.tensor_copy(out, in_), r=r, w=w)

    def recip(self, out, in_, r, w):
        self.P.op("dve", lambda e: e.reciprocal(out, in_), r=r, w=w)

    def memset(self, ap, v, w, eng="pool"):
        self.P.op(eng, lambda e: e.memset(ap, v), w=w)

    def dma(self, out, in_, r, w, q="sp", slow=False):
        if slow:
            return self.P.op(q, lambda e: e.dma_start(out=out, in_=in_, allow_slow_non_contiguous=True), r=r, w=w, dma=True)
        return self.P.op(q, lambda e: e.dma_start(out=out, in_=in_), r=r, w=w, dma=True)

    def fence(self):
        fs = self.fscr
        self.P.fence(lambda e: e.memset(fs[:], 0.0))

    def setup(self):
        nc = self.nc
        self.hT = self.sb("hT", [128, 8, TT], F32)
        self.identf = self.sb("identf", [128, 128], F32)
        self.identb = self.sb("identb", [128, 128], BF16)
        self.onesf = self.sb("onesf", [128, 128], F32)
        self.ones1 = self.sb("ones1", [128, 128], BF16)
        self.ones1024 = self.sb("ones1024", [128, 128], BF16)
        self.ones256 = self.sb("ones256", [128, 128], BF16)
        self.ones128 = self.sb("ones128", [128, 128], BF16)
        self.ones64b = self.sb("ones64b", [128, 128], BF16)
        self.epsc = self.sb("epsc", [128, 1], F32)
        self.fscr = self.sb("fscr", [128, 1], F32)
        self.gains = self.sb("gains", [128, 96], F32)
        self.cb = self.sb("cb", [128, 16], F32)
        self.sq = self.sb("sq", [128, 8, 512], BF16)
        self.rs = self.sb("rs", [128, 2, 512], F32)
        self.arena0 = self.off
        self.ps = [self.es.enter_context(nc.psum_tensor("ps%d" % i, [128, 512], F32)) for i in range(7)]
        self.psb = self.es.enter_context(nc.psum_tensor("psb", [128, 1024], BF16))
        self.rs_i = 0

        m = self.memset
        m(self.identf[:], 0.0, ["identf"])
        idf = self.identf
        self.P.op("pool", lambda e: e.affine_select(out=idf[:], in_=idf[:], pattern=[[-1, 128]],
                                                    compare_op=ALU.not_equal, fill=1.0, base=0,
                                                    channel_multiplier=1), r=["identf"], w=["identf"])
        self.cp(self.identb[:], self.identf[:], ["identf"], ["identb"])
        m(self.onesf[:], 1.0, ["onesf"])
        m(self.ones1[:], 1.0, ["ones1"])
        m(self.ones1024[:], 1.0 / 1024, ["ones1024"])
        m(self.ones256[:], 1.0 / 256, ["ones256"])
        m(self.ones128[:], 1.0 / 128, ["ones128"])
        m(self.ones64b[:], 0.0, ["ones64b"])
        m(self.ones64b[0:64, 0:64], 1.0 / 64, ["ones64b"])
        m(self.ones64b[64:128, 64:128], 1.0 / 64, ["ones64b"])
        m(self.epsc[:], EPS, ["epsc"])
        m(self.fscr[:], 0.0, ["fscr"])

        a0 = self.off
        gtmp = self.sb("gtmp", [96, 128], F32)
        for si, nm in enumerate(("ln_ffn1", "ln_mix", "ln_ffn2")):
            src = self.dh[nm].ap().rearrange("l (c p) -> (l c) p", p=128)
            self.dma(gtmp[32 * si:32 * si + 32, :], src, [], ["gtmp%d" % si])
        self.tr(self.ps[5][:, 0:96], gtmp[:, :], self.identf[0:96, 0:96], ["gtmp0", "gtmp1", "gtmp2", "identf"], ["ps5"])
        self.cp(self.gains[:], self.ps[5][:, 0:96], ["ps5"], ["gains"])
        src = bass.AP(tensor=self.dh["rel_bias"], offset=15 * 16, ap=[[0, 128], [1, 16]])
        self.dma(self.cb[:], src, [], ["cb"])
        relb = self.sb("relb", [32, 16], F32)
        relbc = self.sb("relbc", [32, 16, 128], F32)
        tabE = self.sb("tabE", [32, LT], F32)
        trep = self.sb("trep", [128, 2, LT], F32)
        self.dma(relb[:], self.dh["rel_bias"].ap(), [], ["relb"])
        self.dma(tabE[:], self.dh["tabE"].ap(), [], ["tabE"])
        for h in range(16):
            self.cp(relbc[:, h, :], relb[:, h:h + 1].to_broadcast([32, 128]), ["relb"], ["relbc%d" % h])
        self.toep_ready = []
        for h in range(16):
            tb = h % 2
            for ci, (a, b) in enumerate(((0, 512), (512, 1024), (1024, LT))):
                bank = 5 + (ci % 2)
                self.mm(self.ps[bank][:, 0:b - a], relbc[:, h, :], tabE[:, a:b], True, True,
                        ["relbc%d" % h, "tabE"], ["ps%d" % bank])
                self.cp(trep[:, tb, a:b], self.ps[bank][:, 0:b - a], ["ps%d" % bank], ["trep%d" % tb],
                        eng="act" if ci % 2 else "dve")
            self.dma(self.R.ap()[h], trep[:, tb, :], ["trep%d" % tb], ["R%d" % h])
        self.off = a0

    def gain(self, si, layer):
        o = si * 32 + layer * 8
        return self.gains[:, o:o + 8]

    def load_x(self):
        a0 = self.off
        xtok = self.sb("xtok", [128, 2, D], F32)
        mtok = self.sb("mtok", [16, D], F32)
        x = self.dh["x"].ap()
        self.dma(mtok[:], self.dh["meta_tokens"].ap(), [], ["mtok"])
        for c in range(8):
            self.tr(self.ps[4][:, c * 16:(c + 1) * 16], mtok[0:16, c * 128:(c + 1) * 128], self.identf[0:16, 0:16],
                    ["mtok", "identf"], ["ps4"])
        self.cp(self.hT[:, :, 0:16], self.ps[4][:, 0:128].rearrange("p (c t) -> p c t", c=8), ["ps4"],
                [("h", 0, c) for c in range(8)])
        for i in range(16):
            b = i % 2
            self.dma(xtok[:, b, :], x[128 * i:128 * (i + 1), :], [], ["xtok%d" % b])
            col = 16 + 128 * i
            tile = 1 + i // 4
            for half in range(2):
                bank = 5 + half
                for cc in range(4):
                    c = 4 * half + cc
                    self.tr(self.ps[bank][:, cc * 128:(cc + 1) * 128], xtok[:, b, c * 128:(c + 1) * 128], self.identf[:],
                            ["xtok%d" % b, "identf"], ["ps%d" % bank])
                self.cp(self.hT[:, 4 * half:4 * half + 4, col:col + 128],
                        self.ps[bank][:, :].rearrange("p (c t) -> p c t", c=4), ["ps%d" % bank],
                        [("h", tile, 4 * half + cc) for cc in range(4)], eng="act" if half else "dve")
        self.off = a0

    def store_out(self):
        a0 = self.off
        otok = self.sb("otok", [128, 2, D], F32)
        out = self.out_h.ap()
        fin = []
        for i in range(16):
            b = i % 2
            col = 16 + 128 * i
            tile = 1 + i // 4
            for half in range(2):
                bank = 5 + half
                for cc in range(4):
                    c = 4 * half + cc
                    self.tr(self.ps[bank][:, cc * 128:(cc + 1) * 128], self.hT[:, c, col:col + 128], self.identf[:],
                            [("h", tile, c), "identf"], ["ps%d" % bank])
                self.cp(otok[:, b, half * 512:(half + 1) * 512], self.ps[bank][:, :], ["ps%d" % bank],
                        ["otok%d_%d" % (b, half)], eng="act" if half else "dve")
            fin.append(self.dma(out[128 * i:128 * (i + 1), :], otok[:, b, :], ["otok%d_0" % b, "otok%d_1" % b], []))
        self.off = a0
        return fin

    def rstd_from_ps(self, psN, pkey, n, part=128):
        i = self.rs_i
        self.rs_i = (i + 1) % 2
        rsv = self.rs[0:part, i, 0:n]
        key = "rs%d" % i
        self.act(rsv, psN, AF.Sqrt, [pkey, "epsc"], [key], bias=self.epsc[0:part, 0:1])
        self.recip(rsv, rsv, [key], [key])
        return rsv, key

    def rmsnorm(self, gain, xn):
        for ti, (a, b) in enumerate(TILES):
            n = b - a
            hk = [("h", ti, c) for c in range(8)]
            self.act(self.sq[:, :, 0:n], self.hT[:, :, a:b], AF.Square, hk, [("sq", c) for c in range(8)])
            for c in range(8):
                self.mm(self.ps[6][:, 0:n], self.ones1024[:], self.sq[:, c, 0:n], c == 0, c == 7,
                        [("sq", c), "ones1024"], ["ps6"])
            rsv, key = self.rstd_from_ps(self.ps[6][:, 0:n], "ps6", n)
            for c in range(8):
                self.stt(xn[:, c, a:b], self.hT[:, c, a:b], gain[:, c:c + 1], rsv, ALU.mult, ALU.mult,
                         [("h", ti, c), key, "gains"], [("xn", ti, c)])

    def ffn(self, layer, which):
        self.fence()
        a0 = self.off = self.arena0
        xn = self.sb("xn", [128, 8, TT], BF16)
        gT = self.sb("gT", [128, NFC, 1040], BF16)
        wa = self.sb("wa", [128, 2, 8, 256], BF16)
        wb = self.sb("wb", [128, 2, 8, 256], BF16)
        wo = self.sb("wo", [128, 2, NFC, 256], BF16)
        sa = self.sb("sa", [128, 2, 512], BF16)
        wi_d = self.dh["ffn%d_wi" % which].ap()[layer].rearrange("(c p) f -> p c f", p=128)
        wo_d = self.dh["ffn%d_wo" % which].ap()[layer].rearrange("(c p) f -> p c f", p=128)
        gain = self.gain(0 if which == 1 else 2, layer)
        self.rmsnorm(gain, xn)
        cnt = 0
        wcnt = 0
        for grp in ([0, 1, 2], [3, 4]):
            g0 = TILES[grp[0]][0]
            for fg in range(NFC // 2):
                wbuf = wcnt % 2
                wcnt += 1
                self.dma(wa[:, wbuf], wi_d[:, :, fg * 256:(fg + 1) * 256], [], ["wa%d" % wbuf], q="pool")
                self.dma(wb[:, wbuf], wi_d[:, :, DFF + fg * 256:DFF + (fg + 1) * 256], [], ["wb%d" % wbuf], q="pool")
                for fs in range(2):
                    fc = 2 * fg + fs
                    for ti in grp:
                        a, b = TILES[ti]
                        n = b - a
                        x = cnt % 2
                        cnt += 1
                        pA, pB = self.ps[x], self.ps[2 + x]
                        for c in range(8):
                            self.mm(pA[:, 0:n], wa[:, wbuf, c, fs * 128:(fs + 1) * 128], xn[:, c, a:b], c == 0, c == 7,
                                    ["wa%d" % wbuf, ("xn", ti, c)], ["ps%d" % x])
                        for c in range(8):
                            self.mm(pB[:, 0:n], wb[:, wbuf, c, fs * 128:(fs + 1) * 128], xn[:, c, a:b], c == 0, c == 7,
                                    ["wb%d" % wbuf, ("xn", ti, c)], ["ps%d" % (2 + x)])
                        self.act(sa[:, x, 0:n], pA[:, 0:n], AF.Silu, ["ps%d" % x], ["sa%d" % x])
                        self.tt(gT[:, fc, a - g0:b - g0], sa[:, x, 0:n], pB[:, 0:n], ALU.mult,
                                ["sa%d" % x, "ps%d" % (2 + x)], [("g", fc, ti)])
            ocnt = 0
            for dg in range(4):
                obuf = dg % 2
                self.dma(wo[:, obuf], wo_d[:, :, dg * 256:(dg + 1) * 256], [], ["wo%d" % obuf], q="pool")
                for ds_ in range(2):
                    dc = 2 * dg + ds_
                    for ti in grp:
                        a, b = TILES[ti]
                        n = b - a
                        bank = 4 + ocnt % 2
                        ocnt += 1
                        pO = self.ps[bank]
                        for fc in range(NFC):
                            self.mm(pO[:, 0:n], wo[:, obuf, fc, ds_ * 128:(ds_ + 1) * 128], gT[:, fc, a - g0:b - g0],
                                    fc == 0, fc == NFC - 1, ["wo%d" % obuf, ("g", fc, ti)], ["ps%d" % bank])
                        self.stt(self.hT[:, dc, a:b], pO[:, 0:n], 0.5, self.hT[:, dc, a:b], ALU.mult, ALU.add,
                                 ["ps%d" % bank, ("h", ti, dc)], [("h", ti, dc)])
        self.off = a0

    def group_norm_T(self, psrc, pkey, n, ones, okey, gain_ap, dst, dkeys, part=128):
        self.act(self.sq[0:part, 0, 0:n], psrc, AF.Square, [pkey], [("sq", 0)])
        self.mm(self.ps[3][0:part, 0:n], ones[0:part, 0:part], self.sq[0:part, 0, 0:n], True, True, [("sq", 0), okey], ["ps3"])
        rsv, key = self.rstd_from_ps(self.ps[3][0:part, 0:n], "ps3", n, part)
        if gain_ap is None:
            self.tt(dst, psrc, rsv, ALU.mult, [pkey, key], dkeys)
        else:
            self.stt(dst, psrc, gain_ap, rsv, ALU.mult, ALU.mult, [pkey, key, "gsm"], dkeys)

    def load_strip(self, strip, skey, mstrip, mkey, col, mask, maskkey):
        for s in range(5):
            src = bass.AP(tensor=self.R, offset=col * 128 * LT + 639 - 128 * s, ap=[[LT - 1, 128], [1, 512]])
            self.dma(strip[:, s, :], src, ["R%d" % col], [skey + "_%d" % s])
        src = bass.AP(tensor=self.R, offset=col * 128 * LT + 511, ap=[[LT - 1, 16], [1, 528]])
        self.dma(mstrip[0:16, :], src, ["R%d" % col], [mkey])
        if mask is not None:
            self.tt(strip[:, :, :], strip[:, :, :], mask[:, :, :], ALU.add,
                    [skey + "_%d" % s for s in range(5)] + [maskkey], [skey + "_%d" % s for s in range(5)], eng="pool")

    def ktlist(self, ti, strip, skey, mstrip, mkey, far=True):
        if ti == 0:
            return [(0, mstrip[0:16, 0:16], mkey)]
        jj = ti - 1
        out = [(0, mstrip[0:16, 16:528] if jj == 0 else None, mkey)]
        for kt in range(0, 4 * jj + 4):
            if kt <= 4 * jj - 2:
                if far:
                    out.append((1 + kt, None, None))
            else:
                s = kt - (4 * jj - 1)
                out.append((1 + kt, strip[:, s, :], skey + "_%d" % s))
        return out

    def attn_q(self, ti, kts, s_ops, s_keys, v_ops, v_keys, psO_keys, den_ap, den_key, den_m, scale, cbcol, tmp, PT,
               dyn=None, sbanks=(0, 1), depth=1):
        a, b = TILES[ti]
        nq = b - a
        nk_tot = len(kts)
        nb = len(sbanks)

        def emit_S(idx):
            kt = kts[idx][0]
            ka, kb = KTILES[kt]
            bank = sbanks[idx % nb]
            pS = self.ps[bank][0:kb - ka, 0:nq]
            pairs = s_ops(kt)
            for i, (l, r_) in enumerate(pairs):
                self.mm(pS, l, r_, i == 0, i == len(pairs) - 1, s_keys(kt), ["ps%d" % bank])

        def emit_rest(idx):
            kt, strip, skey = kts[idx]
            ka, kb = KTILES[kt]
            nk = kb - ka
            bank = sbanks[idx % nb]
            pkey = "ps%d" % bank
            pS = self.ps[bank][0:nk, 0:nq]
            pt = PT[0:nk, idx % 3, 0:nq]
            ptk = "PT%d" % (idx % 3)
            dk = None
            if dyn is not None:
                dap, dk = dyn(kt)
            if strip is not None or dyn is not None:
                tb = tmp[0:nk, idx % nb, 0:nq]
                tk = "tmp%d" % (idx % nb)
                if strip is not None:
                    self.stt(tb, pS, scale, strip[0:nk, 0:nq], ALU.mult, ALU.add, [pkey, skey], [tk])
                    if dyn is not None:
                        self.tt(tb, tb, dap, ALU.add, [tk, dk], [tk])
                    self.act(pt, tb, AF.Exp, [tk], [ptk])
                else:
                    self.stt(tb, pS, scale, dap, ALU.mult, ALU.add, [pkey, dk], [tk])
                    self.act(pt, tb, AF.Exp, [tk, "cb"], [ptk], bias=self.cb[0:nk, cbcol:cbcol + 1])
            else:
                self.act(pt, pS, AF.Exp, [pkey, "cb"], [ptk], bias=self.cb[0:nk, cbcol:cbcol + 1], scale=scale)
            for (l, o), okey in zip(v_ops(kt), psO_keys):
                self.mm(o, l, pt, idx == 0, idx == nk_tot - 1, [ptk] + v_keys(kt), [okey])
            self.mm(den_ap, self.ones1[0:nk, 0:den_m], pt, idx == 0, idx == nk_tot - 1, [ptk, "ones1"], [den_key])

        for idx in range(min(depth, nk_tot)):
            emit_S(idx)
        for idx in range(nk_tot):
            if idx + depth < nk_tot:
                emit_S(idx + depth)
            emit_rest(idx)

    def wout_acc(self, wo_ap, wokey, OT, otkey, ti, nchunks=1):
        a, b = TILES[ti]
        n = b - a
        for dc in range(8):
            bank = 5 + dc % 2
            for c in range(nchunks):
                l = wo_ap(c, dc)
                r_ = OT(c)
                self.mm(self.ps[bank][:, 0:n], l, r_, c == 0, c == nchunks - 1, [wokey, otkey], ["ps%d" % bank])
            self.tt(self.hT[:, dc, a:b], self.ps[bank][:, 0:n], self.hT[:, dc, a:b], ALU.add,
                    ["ps%d" % bank, ("h", ti, dc)], [("h", ti, dc)])

    def proj_tokmajor(self, xn, wv, wkey, V, ncols, vkey):
        for kt, (ka, kb) in enumerate(KTILES):
            nk = kb - ka
            ti = 0 if kt == 0 else 1 + (kt - 1) // 4
            bank = 5 + kt % 2
            for c in range(8):
                self.mm(self.ps[bank][0:nk, 0:ncols], xn[:, c, ka:kb], wv[:, c, :], c == 0, c == 7,
                        [("xn", ti, c), wkey], ["ps%d" % bank])
            self.cp(V[0:nk, kt, :], self.ps[bank][0:nk, 0:ncols], ["ps%d" % bank], [(vkey, kt)], eng="act")

    def diff_attn(self, layer, j):
        self.fence()
        a0 = self.off = self.arena0
        xn = self.sb("xn", [128, 8, TT], BF16)
        wq = self.sb("wq", [128, 2, 8, 128], BF16)
        wk = self.sb("wk", [128, 2, 8, 128], BF16)
        wv = self.sb("wv", [128, 2, 8, 128], BF16)
        woh = self.sb("woh", [128, 2, D], BF16)
        QT = self.sb("QT", [128, TT], BF16)
        KT = self.sb("KT", [128, TT], BF16)
        V = self.sb("V", [128, 17, 128], BF16)
        strip = self.sb("strip", [128, 2, 5, 512], F32)
        mstrip = self.sb("mstrip", [16, 2, 528], F32)
        maskD = self.sb("maskD", [128, 5, 512], F32)
        o0 = self.sb("o0", [128, TT], F32)
        tmp = self.sb("tmp", [128, 3, 512], F32)
        PT = self.sb("PT", [128, 3, 512], BF16)
        rden = self.sb("rden", [128, 512], F32)
        ob = self.sb("ob", [128, 2, 512], F32)
        OT = self.sb("OT", [128, 2, 512], BF16)
        gsm = self.sb("gsm", [128, 4], F32)
        lam4 = self.sb("lam4", [64, 4], F32)
        lsc = self.sb("lsc", [128, 4], F32)
        lambda_init = 0.8 - 0.6 * math.exp(-0.3 * layer)
        scale = 64 ** -0.5

        self.rmsnorm(self.gain(1, layer), xn)
        qkn = self.dh["a_qk_norm"].ap()[j]
        for half in range(2):
            for qi in range(2):
                self.dma(gsm[64 * half:64 * half + 64, qi:qi + 1], qkn[qi].rearrange("(p o) -> p o", o=1), [], ["gsm"])
        self.dma(gsm[:, 2:3], self.dh["a_subln"].ap()[j].rearrange("(p o) -> p o", o=1), [], ["gsm"])
        self.ts(gsm[:, 2:3], gsm[:, 2:3], 1.0 - lambda_init, None, ALU.mult, ["gsm"], ["gsm"])
        self.dma(lam4[:], self.dh["a_lambda"].ap()[j].rearrange("f p -> p f"), [], ["lam4"], slow=True)
        self.tt(lsc[0:64, 0:1], lam4[:, 0:1], lam4[:, 1:2], ALU.mult, ["lam4"], ["lsc"])
        self.tt(lsc[0:64, 1:2], lam4[:, 2:3], lam4[:, 3:4], ALU.mult, ["lam4", "lsc"], ["lsc"])
        self.mm(self.ps[3][:, 0:2], self.onesf[0:64, :], lsc[0:64, 0:2], True, True, ["lsc", "onesf"], ["ps3"])
        self.act(lsc[:, 2:4], self.ps[3][:, 0:2], AF.Exp, ["ps3", "lsc"], ["lsc"])
        self.tt(gsm[:, 3:4], lsc[:, 3:4], lsc[:, 2:3], ALU.subtract, ["lsc", "gsm"], ["gsm"])
        self.ts(gsm[:, 3:4], gsm[:, 3:4], -lambda_init, None, ALU.add, ["gsm"], ["gsm"])
        self.dma(maskD[:], self.dh["maskD"].ap(), [], ["maskD"])
        w_in = self.dh["a_w_in"].ap()[j].rearrange("(c p) f -> p c f", p=128)
        w_o = self.dh["w_out"].ap()[layer]
        mapcnt = 0
        pending = []
        for h in range(8):
            wb_ = h % 2
            self.dma(wq[:, wb_], w_in[:, :, h * 128:(h + 1) * 128], [], ["wq%d" % wb_], q="pool")
            self.dma(wk[:, wb_], w_in[:, :, 1024 + h * 128:1024 + (h + 1) * 128], [], ["wk%d" % wb_], q="pool")
            self.dma(wv[:, wb_], w_in[:, :, 2048 + h * 128:2048 + (h + 1) * 128], [], ["wv%d" % wb_], q="pool")
            self.dma(woh[:, wb_, :], w_o[h * 128:(h + 1) * 128, :], [], ["woh%d" % wb_], q="pool")
            for (wt, wkey, dst, dname, gcol) in ((wq, "wq%d" % wb_, QT, "QT", 0), (wk, "wk%d" % wb_, KT, "KT", 1)):
                for ti, (a, b) in enumerate(TILES):
                    n = b - a
                    bank = 5 + ti % 2
                    for c in range(8):
                        self.mm(self.ps[bank][:, 0:n], wt[:, wb_, c, :], xn[:, c, a:b], c == 0, c == 7,
                                [wkey, ("xn", ti, c)], ["ps%d" % bank])
                    self.group_norm_T(self.ps[bank][:, 0:n], "ps%d" % bank, n, self.ones64b, "ones64b",
                                      gsm[:, gcol:gcol + 1], dst[:, a:b], [(dname, ti)])
            self.proj_tokmajor(xn, wv[:, wb_], "wv%d" % wb_, V, 128, "V")
            for m in range(2):
                sbuf_i = mapcnt % 2
                mapcnt += 1
                col = m * 8 + h
                skey, mkey = "strip%d" % sbuf_i, "mstrip%d" % sbuf_i
                self.load_strip(strip[:, sbuf_i], skey, mstrip[:, sbuf_i], mkey, col, maskD, "maskD")
                for ti, (a, b) in enumerate(TILES):
                    nq = b - a
                    kts = self.ktlist(ti, strip[:, sbuf_i], skey, mstrip[:, sbuf_i], mkey)

                    def s_ops(kt, m=m, a=a, b=b):
                        ka, kb = KTILES[kt]
                        return [(KT[64 * m:64 * m + 64, ka:kb], QT[64 * m:64 * m + 64, a:b])]

                    def s_keys(kt, ti=ti):
                        kti = 0 if kt == 0 else 1 + (kt - 1) // 4
                        return [("KT", kti), ("QT", ti)]

                    def v_ops(kt, nq=nq):
                        ka, kb = KTILES[kt]
                        return [(V[0:kb - ka, kt, :], self.ps[2][:, 0:nq])]

                    def v_keys(kt):
                        return [("V", kt)]

                    self.attn_q(ti, kts, s_ops, s_keys, v_ops, v_keys, ["ps2"], self.ps[4][:, 0:nq], "ps4", 128,
                                scale, col, tmp, PT, sbanks=(0, 1, 3), depth=2)
                    for f in pending:
                        f()
                    pending = []
                    self.recip(rden[:, 0:nq], self.ps[4][:, 0:nq], ["ps4"], ["rden"])
                    if m == 0:
                        self.tt(o0[:, a:b], self.ps[2][:, 0:nq], rden[:, 0:nq], ALU.mult, ["ps2", "rden"], [("o0", ti)])
                    else:
                        x = ti % 2
                        self.tt(ob[:, x, 0:nq], self.ps[2][:, 0:nq], rden[:, 0:nq], ALU.mult, ["ps2", "rden"], ["ob%d" % x])
                        self.stt(ob[:, x, 0:nq], ob[:, x, 0:nq], gsm[:, 3:4], o0[:, a:b], ALU.mult, ALU.add,
                                 ["ob%d" % x, ("o0", ti), "gsm"], ["ob%d" % x])

                        def tail(x=x, nq=nq, ti=ti, wb_=wb_):
                            self.group_norm_T(ob[:, x, 0:nq], "ob%d" % x, nq, self.ones128, "ones128", gsm[:, 2:3],
                                              OT[:, x, 0:nq], ["OT%d" % x])
                            self.wout_acc(lambda c, dc: woh[:, wb_, dc * 128:(dc + 1) * 128], "woh%d" % wb_,
                                          lambda c: OT[:, x, 0:nq], "OT%d" % x, ti)
                        pending.append(tail)
                for f in pending:
                    f()
                pending = []
        self.off = a0

    def swa_attn(self, layer):
        self.fence()
        a0 = self.off = self.arena0
        xn = self.sb("xn", [128, 8, TT], BF16)
        wq = self.sb("wq", [128, 2, 8, 128], BF16)
        wkd = self.sb("wkd", [128, 8, 2, 128], BF16)
        wv = self.sb("wv", [128, 8, 128], BF16)
        woh = self.sb("woh", [128, 2, D], BF16)
        QT = self.sb("QT", [128, TT], BF16)
        KT = self.sb("KT", [128, 2, TT], BF16)
        V = self.sb("V", [128, 17, 128], BF16)
        strip = self.sb("strip", [128, 2, 5, 512], F32)
        mstrip = self.sb("mstrip", [16, 2, 528], F32)
        maskW = self.sb("maskW", [128, 5, 512], F32)
        tmp = self.sb("tmp", [128, 3, 512], F32)
        PT = self.sb("PT", [128, 3, 512], BF16)
        rden = self.sb("rden", [128, 512], F32)
        OT = self.sb("OT", [128, 2, 512], BF16)
        gsm = self.sb("gsm", [128, 4], F32)
        esk = self.sb("esk", [128, 8], F32)
        scale = 64 ** -0.5

        self.rmsnorm(self.gain(1, layer), xn)
        qkn = self.dh["c_qk_norm"].ap()[0]
        for half in range(2):
            for qi in range(2):
                self.dma(gsm[64 * half:64 * half + 64, qi:qi + 1], qkn[qi].rearrange("(p o) -> p o", o=1), [], ["gsm"])
        for half in range(2):
            src = bass.AP(tensor=self.dh["c_sinks"], offset=half, ap=[[0, 64], [2, 8]])
            self.dma(esk[64 * half:64 * half + 64, :], src, [], ["esk"], slow=True)
        self.act(esk[:], esk[:], AF.Exp, ["esk"], ["esk"])
        self.dma(maskW[:], self.dh["maskW"].ap(), [], ["maskW"])
        w_in = self.dh["c_w_in"].ap()[0].rearrange("(c p) f -> p c f", p=128)
        w_o = self.dh["w_out"].ap()[layer]
        for g in range(2):
            for half in range(2):
                self.dma(wkd[:, :, g, 64 * half:64 * half + 64], w_in[:, :, 1024 + 64 * g:1024 + 64 * g + 64], [],
                         ["wkd"], q="pool")
        self.dma(wv[:], w_in[:, :, 1152:1280], [], ["wv"], q="pool")
        for g in range(2):
            for ti, (a, b) in enumerate(TILES):
                n = b - a
                bank = 5 + ti % 2
                for c in range(8):
                    self.mm(self.ps[bank][:, 0:n], wkd[:, c, g, :], xn[:, c, a:b], c == 0, c == 7,
                            ["wkd", ("xn", ti, c)], ["ps%d" % bank])
                self.group_norm_T(self.ps[bank][:, 0:n], "ps%d" % bank, n, self.ones64b, "ones64b", gsm[:, 1:2],
                                  KT[:, g, a:b], [("KT", g, ti)])
        self.proj_tokmajor(xn, wv, "wv", V, 128, "V")
        pending = []
        for cpair in range(8):
            wb_ = cpair % 2
            g = cpair // 4
            self.dma(wq[:, wb_], w_in[:, :, cpair * 128:(cpair + 1) * 128], [], ["wq%d" % wb_], q="pool")
            self.dma(woh[:, wb_, :], w_o[cpair * 128:(cpair + 1) * 128, :], [], ["woh%d" % wb_], q="pool")
            for ti, (a, b) in enumerate(TILES):
                n = b - a
                bank = 5 + ti % 2
                for c in range(8):
                    self.mm(self.ps[bank][:, 0:n], wq[:, wb_, c, :], xn[:, c, a:b], c == 0, c == 7,
                            ["wq%d" % wb_, ("xn", ti, c)], ["ps%d" % bank])
                self.group_norm_T(self.ps[bank][:, 0:n], "ps%d" % bank, n, self.ones64b, "ones64b", gsm[:, 0:1],
                                  QT[:, a:b], [("QT", ti)])
            for s in range(2):
                self.load_strip(strip[:, s], "strip%d" % s, mstrip[:, s], "mstrip%d" % s, 2 * cpair + s, maskW, "maskW")
            for ti, (a, b) in enumerate(TILES):
                nq = b - a
                for s in range(2):
                    kts = self.ktlist(ti, strip[:, s], "strip%d" % s, mstrip[:, s], "mstrip%d" % s, far=False)

                    def s_ops(kt, s=s, a=a, b=b, g=g):
                        ka, kb = KTILES[kt]
                        return [(KT[64 * s:64 * s + 64, g, ka:kb], QT[64 * s:64 * s + 64, a:b])]

                    def s_keys(kt, ti=ti, g=g):
                        kti = 0 if kt == 0 else 1 + (kt - 1) // 4
                        return [("KT", g, kti), ("QT", ti)]

                    def v_ops(kt, nq=nq, s=s, g=g):
                        ka, kb = KTILES[kt]
                        return [(V[0:kb - ka, kt, 64 * g:64 * g + 64], self.ps[2][64 * s:64 * s + 64, 0:nq])]

                    def v_keys(kt):
                        return [("V", kt)]

                    self.attn_q(ti, kts, s_ops, s_keys, v_ops, v_keys, ["ps2_%d" % s],
                                self.ps[4][64 * s:64 * s + 64, 0:nq], "ps4_%d" % s, 64, scale, 2 * cpair + s, tmp, PT,
                                sbanks=(0, 1, 3), depth=2)
                for f in pending:
                    f()
                pending = []
                x = ti % 2
                self.ts(rden[:, 0:nq], self.ps[4][:, 0:nq], esk[:, cpair:cpair + 1], None, ALU.add,
                        ["ps4_0", "ps4_1", "esk"], ["rden"])
                self.recip(rden[:, 0:nq], rden[:, 0:nq], ["rden"], ["rden"])
                self.tt(OT[:, x, 0:nq], self.ps[2][:, 0:nq], rden[:, 0:nq], ALU.mult, ["ps2_0", "ps2_1", "rden"],
                        ["OT%d" % x])
                def tail(x=x, nq=nq, ti=ti, wb_=wb_):
                    self.wout_acc(lambda c, dc: woh[:, wb_, dc * 128:(dc + 1) * 128], "woh%d" % wb_,
                                  lambda c: OT[:, x, 0:nq], "OT%d" % x, ti)
                pending.append(tail)
            for f in pending:
                f()
            pending = []
        self.off = a0

    def dsa_attn(self, layer):
        self.fence()
        a0 = self.off = self.arena0
        P = self.P
        xn = self.sb("xn", [128, 8, TT], BF16)
        after_xn = self.off
        early0 = self.off
        w1 = self.sb("w1", [128, 8, 512], BF16)
        wki = self.sb("wki", [128, 8, 128], BF16)
        wwi = self.sb("wwi", [128, 8, 8], BF16)
        early1 = self.off
        self.off = early0
        work = self.sb("work", [128, TT], F32)
        qabs = self.sb("qabs", [128, 2, 512], BF16)
        assert self.off <= early1
        self.off = early1
        wuq = self.sb("wuq", [128, 2, 2, 256], BF16)
        wqi = self.sb("wqi", [128, 2, 512], BF16)
        wuv = self.sb("wuv", [128, 16, 2, 64], BF16)
        wo = self.sb("wo", [128, 2, 8, 128], BF16)
        cqT = self.sb("cqT", [128, 2, TT], BF16)
        ckvT = self.sb("ckvT", [128, 2, TT], BF16)
        ckv = self.sb("ckv", [128, 17, 256], BF16)
        kiT = self.sb("kiT", [128, TT], BF16)
        qiT = self.sb("qiT", [128, 4, 512], BF16)
        wsc = self.sb("wsc", [128, 17, 8], F32)
        strip = self.sb("strip", [128, 5, 512], F32)
        mstrip = self.sb("mstrip", [16, 528], F32)
        tmp = self.sb("tmp", [128, 2, 512], F32)
        rden = self.sb("rden", [128, 512], F32)
        olat = self.sb("olat", [128, 2, 512], BF16)
        oq = self.sb("oq", [128, 8, 512], BF16)
        relu = self.sb("relu", [128, 2, 512], F32)
        cm = self.sb("cm", [128, 128], F32)
        gsm = self.sb("gsm", [128, 8], F32)
        mx = self.sb("mx", [128, 8], F32)
        thr = self.sb("thr", [128, 1], F32)
        end_arena = self.off
        self.off = self.arena0
        maskT = self.sb("maskT", [128, 17, 512], BF16)
        score = self.sb("score", [128, TT], F32)
        m01 = self.sb("m01", [128, TT], BF16)
        PT = self.sb("PT", [128, 3, 512], BF16)
        assert self.off <= after_xn
        self.off = end_arena

        self.rmsnorm(self.gain(1, layer), xn)
        w_in = self.dh["b_w_in"].ap()[0].rearrange("(c p) f -> p c f", p=128)
        self.dma(w1[:], w_in[:, :, 0:512], [], ["w1"], q="pool")
        for half in range(2):
            self.dma(wki[:, :, 64 * half:64 * half + 64], w_in[:, :, 512:576], [], ["wki"], q="pool")
        self.dma(wwi[:], w_in[:, :, 576:584], [], ["wwi"], q="pool")
        ln = self.dh["b_latent_norm"].ap()[0]
        for i in range(2):
            self.dma(gsm[:, 2 * i:2 * i + 2], ln[i].rearrange("(c p) -> p c", p=128), [], ["gsm"], slow=True)
        self.dma(gsm[:, 4:6], self.dh["b_q_norm"].ap()[0].rearrange("(c p) -> p c", p=128), [], ["gsm"], slow=True)
        self.dma(cm[:], self.dh["cm128"].ap(), [], ["cm"])
        uv = self.dh["b_w_uv"].ap()[0]
        for h in range(16):
            self.dma(wuv[:, h], uv[h].rearrange("(c p) e -> p c e", p=128), [], ["wuv"], q="pool")
        w_uq = self.dh["b_w_uq"].ap()[0].rearrange("(c p) f -> p c f", p=128)
        self.dma(wqi[:], w_uq[:, :, 4096:4608], [], ["wqi"], q="pool")
        w_o = self.dh["w_out"].ap()[layer].rearrange("(c p) f -> p c f", p=128)

        for ti, (a, b) in enumerate(TILES):
            n = b - a
            for which, dst, dname in ((0, cqT, "cqT"), (1, ckvT, "ckvT")):
                for rc in range(2):
                    bank = 5 + rc
                    for c in range(8):
                        self.mm(self.ps[bank][:, 0:n], w1[:, c, which * 256 + rc * 128:which * 256 + (rc + 1) * 128],
                                xn[:, c, a:b], c == 0, c == 7, ["w1", ("xn", ti, c)], ["ps%d" % bank])
                    self.act(self.sq[:, rc, 0:n], self.ps[bank][:, 0:n], AF.Square, ["ps%d" % bank], [("sq", rc)])
                for rc in range(2):
                    self.mm(self.ps[3][:, 0:n], self.ones256[:], self.sq[:, rc, 0:n], rc == 0, rc == 1,
                            [("sq", rc), "ones256"], ["ps3"])
                rsv, key = self.rstd_from_ps(self.ps[3][:, 0:n], "ps3", n)
                for rc in range(2):
                    self.stt(dst[:, rc, a:b], self.ps[5 + rc][:, 0:n], gsm[:, 2 * which + rc:2 * which + rc + 1], rsv,
                             ALU.mult, ALU.mult, ["ps%d" % (5 + rc), key, "gsm"], [(dname, rc, ti)])
            bank = 5
            for c in range(8):
                self.mm(self.ps[bank][:, 0:n], wki[:, c, :], xn[:, c, a:b], c == 0, c == 7, ["wki", ("xn", ti, c)],
                        ["ps%d" % bank])
            self.group_norm_T(self.ps[bank][:, 0:n], "ps%d" % bank, n, self.ones64b, "ones64b", None, kiT[:, a:b],
                              [("kiT", ti)])
        for kt, (ka, kb) in enumerate(KTILES):
            nk = kb - ka
            ti = 0 if kt == 0 else 1 + (kt - 1) // 4
            bank = 5 + kt % 2
            for c in range(8):
                self.mm(self.ps[bank][0:nk, 0:8], xn[:, c, ka:kb], wwi[:, c, :], c == 0, c == 7,
                        [("xn", ti, c), "wwi"], ["ps%d" % bank])
            self.ts(wsc[0:nk, kt, :], self.ps[bank][0:nk, 0:8], (8 ** -0.5) * (64 ** -0.5), None, ALU.mult,
                    ["ps%d" % bank], [("wsc", kt)])
            for rc in range(2):
                self.tr(self.psb[0:nk, rc * 128:(rc + 1) * 128], ckvT[:, rc, ka:kb], self.identb[:],
                        [("ckvT", rc, ti), "identb"], ["psb"])
            self.cp(ckv[0:nk, kt, :], self.psb[0:nk, 0:256], ["psb"], [("ckv", kt)], eng="act")
        self.fence()
        scale = 256 ** -0.5
        for ti, (a, b) in enumerate(TILES):
            nq = b - a
            jj = ti - 1
            for ch in range(4):
                bank = 5 + ch % 2
                for rc in range(2):
                    self.mm(self.ps[bank][:, 0:nq], wqi[:, rc, ch * 128:(ch + 1) * 128], cqT[:, rc, a:b], rc == 0, rc == 1,
                            ["wqi", ("cqT", rc, ti)], ["ps%d" % bank])
                self.cp(qiT[:, ch, 0:nq], self.ps[bank][:, 0:nq], ["ps%d" % bank], [("qiT", ch)], eng="act")
            if ti == 0:
                self.memset(maskT[0:16, 0, 0:16], 0.0, [("maskT", 0)])
            else:
                self.memset(maskT[:, 4 * jj + 1:4 * jj + 5, :], NEG, [("maskT", 4 * jj + 1 + u) for u in range(4)])
                for sub in range(4):
                    i = 4 * jj + sub
                    qa = a + 128 * sub
                    wd = 16 + 128 * (i + 1)
                    qkt = 1 + i
                    blocks = [(c0, min(c0 + 512, wd)) for c0 in range(0, wd, 512)]
                    for bi, (c0, c1) in enumerate(blocks):
                        nkc = c1 - c0
                        for hh in range(8):
                            bank = 5 + hh % 2
                            s_ = hh % 2
                            self.mm(self.ps[bank][:, 0:nkc], qiT[64 * s_:64 * s_ + 64, hh // 2, 128 * sub:128 * (sub + 1)],
                                    kiT[64 * s_:64 * s_ + 64, c0:c1], True, True,
                                    [("qiT", hh // 2)] + [("kiT", t) for t in range(5)], ["ps%d" % bank])
                            self.act(relu[:, s_, 0:nkc], self.ps[bank][:, 0:nkc], AF.Relu, ["ps%d" % bank], ["relu%d" % s_])
                            if hh == 0:
                                self.ts(score[:, c0:c1], relu[:, s_, 0:nkc], wsc[:, qkt, hh:hh + 1], None, ALU.mult,
                                        ["relu%d" % s_, ("wsc", qkt)], [("score", bi)])
                            else:
                                self.stt(score[:, c0:c1], relu[:, s_, 0:nkc], wsc[:, qkt, hh:hh + 1], score[:, c0:c1],
                                         ALU.mult, ALU.add, ["relu%d" % s_, ("wsc", qkt), ("score", bi)], [("score", bi)])
                    skeys = [("score", bi) for bi in range(len(blocks))]
                    self.tt(score[:, wd - 128:wd], score[:, wd - 128:wd], cm[:], ALU.add, skeys + ["cm"], skeys)
                    src = score
                    for rnd in range(32):
                        self.P.op("dve", lambda e, src=src, wd=wd: e.max(out=mx[:], in_=src[:, 0:wd]),
                                  r=skeys + ["work"], w=["mx"])
                        if rnd < 31:
                            self.P.op("dve", lambda e, src=src, wd=wd: e.match_replace(
                                out=work[:, 0:wd], in_to_replace=mx[:], in_values=src[:, 0:wd], imm_value=NEG),
                                r=skeys + ["mx", "work"], w=["work"])
                            src = work
                    self.ts(thr[:], mx[:, 7:8], -1e29, None, ALU.max, ["mx"], ["thr"])
                    self.ts(m01[:, 0:wd], score[:, 0:wd], thr[:, 0:1], None, ALU.is_ge, skeys + ["thr"], ["m01"])
                    self.tr(self.psb[0:16, 0:128], m01[:, 0:16], self.identb[:], ["m01", "identb"], ["psb"])
                    self.ts(maskT[0:16, 0, 128 * sub:128 * (sub + 1)], self.psb[0:16, 0:128], -1.0, -NEG, ALU.add,
                            ["psb"], [("maskT", 0)], op1=ALU.mult)
                    for k0 in range(1, i + 2, 8):
                        k1 = min(k0 + 8, i + 2)
                        for kt in range(k0, k1):
                            ka, kb = KTILES[kt]
                            self.tr(self.psb[:, (kt - k0) * 128:(kt - k0 + 1) * 128], m01[:, ka:kb], self.identb[:],
                                    ["m01", "identb"], ["psb"])
                        self.ts(maskT[:, k0:k1, 128 * sub:128 * (sub + 1)],
                                self.psb[:, 0:(k1 - k0) * 128].rearrange("p (k q) -> p k q", q=128), -1.0, -NEG, ALU.add,
                                ["psb"], [("maskT", kt) for kt in range(k0, k1)], op1=ALU.mult)
            pending = []
            for h in range(16):
                ub = h % 2
                self.dma(wuq[:, ub], w_uq[:, :, h * 256:(h + 1) * 256], [], ["wuq%d" % ub], q="pool")
                for rc in range(2):
                    bank = 5 + rc
                    for c2 in range(2):
                        self.mm(self.ps[bank][:, 0:nq], wuq[:, ub, c2, rc * 128:(rc + 1) * 128], cqT[:, c2, a:b],
                                c2 == 0, c2 == 1, ["wuq%d" % ub, ("cqT", c2, ti)], ["ps%d" % bank])
                    self.act(self.sq[:, rc, 0:nq], self.ps[bank][:, 0:nq], AF.Square, ["ps%d" % bank], [("sq", rc)])
                for rc in range(2):
                    self.mm(self.ps[3][:, 0:nq], self.ones256[:], self.sq[:, rc, 0:nq], rc == 0, rc == 1,
                            [("sq", rc), "ones256"], ["ps3"])
                rsv, key = self.rstd_from_ps(self.ps[3][:, 0:nq], "ps3", nq)
                for rc in range(2):
                    self.stt(qabs[:, rc, 0:nq], self.ps[5 + rc][:, 0:nq], gsm[:, 4 + rc:5 + rc], rsv, ALU.mult, ALU.mult,
                             ["ps%d" % (5 + rc), key, "gsm"], [("qabs", rc)])
                for f in pending:
                    f()
                pending = []
                self.load_strip(strip, "strip", mstrip, "mstrip", h, None, None)
                kts = self.ktlist(ti, strip, "strip", mstrip, "mstrip")

                def s_ops(kt, nq=nq):
                    ka, kb = KTILES[kt]
                    return [(ckvT[:, rc, ka:kb], qabs[:, rc, 0:nq]) for rc in range(2)]

                def s_keys(kt):
                    kti = 0 if kt == 0 else 1 + (kt - 1) // 4
                    return [("ckvT", 0, kti), ("ckvT", 1, kti), ("qabs", 0), ("qabs", 1)]

                def v_ops(kt, nq=nq):
                    ka, kb = KTILES[kt]
                    return [(ckv[0:kb - ka, kt, rc * 128:(rc + 1) * 128], self.ps[2 + rc][:, 0:nq]) for rc in range(2)]

                def v_keys(kt):
                    return [("ckv", kt)]

                def dyn(kt, nq=nq):
                    ka, kb = KTILES[kt]
                    return maskT[0:kb - ka, kt, 0:nq], ("maskT", kt)

                self.attn_q(ti, kts, s_ops, s_keys, v_ops, v_keys, ["ps2", "ps3"], self.ps[4][:, 0:nq], "ps4", 128,
                            scale, h, tmp, PT, dyn=dyn)
                self.recip(rden[:, 0:nq], self.ps[4][:, 0:nq], ["ps4"], ["rden"])
                for rc in range(2):
                    self.tt(olat[:, rc, 0:nq], self.ps[2 + rc][:, 0:nq], rden[:, 0:nq], ALU.mult,
                            ["ps%d" % (2 + rc), "rden"], [("olat", rc)])
                def tail(h=h, nq=nq):
                    s_ = h % 2
                    bank = 5 + h % 2
                    for rc in range(2):
                        self.mm(self.ps[bank][64 * s_:64 * s_ + 64, 0:nq], wuv[:, h, rc, :], olat[:, rc, 0:nq], rc == 0,
                                rc == 1, ["wuv", ("olat", rc)], ["ps%d" % bank])
                    self.cp(oq[64 * s_:64 * s_ + 64, h // 2, 0:nq], self.ps[bank][64 * s_:64 * s_ + 64, 0:nq],
                            ["ps%d" % bank], [("oq", h // 2, s_)], eng="act")
                pending.append(tail)
            for f in pending:
                f()
            pending = []
            for dc in range(8):
                ob_ = dc % 2
                self.dma(wo[:, ob_], w_o[:, :, dc * 128:(dc + 1) * 128], [], ["wo%d" % ob_], q="pool")
                bank = 5 + dc % 2
                for c in range(8):
                    self.mm(self.ps[bank][:, 0:nq], wo[:, ob_, c, :], oq[:, c, 0:nq], c == 0, c == 7,
                            ["wo%d" % ob_, ("oq", c, 0), ("oq", c, 1)], ["ps%d" % bank])
                self.tt(self.hT[:, dc, a:b], self.ps[bank][:, 0:nq], self.hT[:, dc, a:b], ALU.add,
                        ["ps%d" % bank, ("h", ti, dc)], [("h", ti, dc)])
        self.off = a0

    def build(self):
        self.setup()
        self.load_x()
        st = self.stages
        for layer in range(DEPTH):
            if st is None or ("f1", layer) in st:
                self.ffn(layer, 1)
            if st is None or ("mix", layer) in st:
                kind, j = layer % 3, layer // 3
                if kind == 0:
                    self.diff_attn(layer, j)
                elif kind == 1:
                    self.dsa_attn(layer)
                else:
                    self.swa_attn(layer)
            if st is None or ("f2", layer) in st:
                self.ffn(layer, 2)
        self.fence()
        fin = self.store_out()
        self.P.emit(self.nc, final_wait_ops=fin)
        self.es.close()
        return self.nc


_TABLES = None


def _in_maps(inputs, cores):
    global _TABLES
    if _TABLES is None:
        _TABLES = _host_tables()
    shared = {}
    for name, shape in IN_SPECS:
        if name == "x":
            continue
        if name in _TABLES:
            shared[name] = _TABLES[name]
        else:
            shared[name] = np.ascontiguousarray(np.asarray(inputs[name], dtype=np.float32))
    x = np.asarray(inputs["x"], dtype=np.float32)
    maps = []
    for b in cores:
        m = dict(shared)
        m["x"] = np.ascontiguousarray(x[b])
        maps.append(m)
    return maps


def kernel(**inputs):
    nc = Builder().build()
    maps = _in_maps(inputs, list(range(8)))
    res = run_bass_kernel_spmd(nc, maps, core_ids=list(range(8)))
    return np.stack([np.asarray(r["out"], dtype=np.float32) for r in res.results], axis=0)
```
